# Optimizing a Trainium2 kernel written in Bass

```python
import math
import jax
import jax.numpy as jnp
from jax import lax
import numpy as np

D_MODEL = 1024
BATCH = 4
SEQ = 8192
DEPTH = 4

BLOCK = 128
HEAD_DIM = 64
N_BRANCH = 4
BRANCH_WIDTH = 512
EPS = 1e-6

A_HEADS = 8
IDX_HEADS = 8
IDX_DIM = 32
TOPK_MAX = 256

B_GROUPS = ((128, 1), (512, 4), (2048, 16))
B_HEADS = 8

C_HEADS = 8
C_NOPE = 64
C_ROPE = 32
C_V = 64
C_Q_LORA = 256
C_KV_LORA = 128
ROPE_THETA = 10000.0

D_HEADS = 8
D_KV_HEADS = 2
D_WINDOW = 128

NUM_BUCKETS = 32
MAX_DISTANCE = 2048
N_BIAS_HEADS = A_HEADS + len(B_GROUPS) * B_HEADS + D_HEADS

A_SIZES = (A_HEADS * HEAD_DIM, A_HEADS * HEAD_DIM, A_HEADS * HEAD_DIM, BRANCH_WIDTH,
           IDX_HEADS * IDX_DIM, IDX_DIM, IDX_HEADS)
B_SIZES = (len(B_GROUPS) * B_HEADS * HEAD_DIM,) * 3 + (BRANCH_WIDTH,)
C_SIZES = (C_Q_LORA, C_KV_LORA, C_ROPE, BRANCH_WIDTH)
D_SIZES = (D_HEADS * HEAD_DIM, D_KV_HEADS * HEAD_DIM, D_KV_HEADS * HEAD_DIM, BRANCH_WIDTH)
GATE_SIZES = (D_MODEL,) * N_BRANCH
IN_GROUPS = (A_SIZES, B_SIZES, C_SIZES, D_SIZES, GATE_SIZES)
N_IN = sum(sum(s) for s in IN_GROUPS)

kernel_name = 'hybrid_parallel_gated_mixers'


def rms_norm(x, gain):
    xf = x.astype(jnp.float32)
    y = xf * lax.rsqrt(jnp.mean(xf * xf, axis=-1, keepdims=True) + EPS)
    return (y * gain.astype(jnp.float32)).astype(x.dtype)


def t5_bucket(dist):
    max_exact = NUM_BUCKETS // 2
    d = jnp.maximum(dist, 0)
    logd = jnp.log(jnp.maximum(d, 1).astype(jnp.float32) / max_exact)
    large = max_exact + (logd / math.log(MAX_DISTANCE / max_exact) * (NUM_BUCKETS - max_exact)).astype(jnp.int32)
    return jnp.where(d < max_exact, d, jnp.minimum(large, NUM_BUCKETS - 1))


def rope(x, pos):
    half = x.shape[-1] // 2
    freq = ROPE_THETA ** (-jnp.arange(half, dtype=jnp.float32) / half)
    ang = pos.astype(jnp.float32)[:, None] * freq[None, :]
    cos, sin = jnp.cos(ang)[:, None, :], jnp.sin(ang)[:, None, :]
    xf = x.astype(jnp.float32)
    x1, x2 = xf[..., :half], xf[..., half:]
    return jnp.concatenate([x1 * cos - x2 * sin, x2 * cos + x1 * sin], axis=-1).astype(x.dtype)


def combined_projection(xn, w):
    groups = []
    start = 0
    for sizes in IN_GROUPS:
        width = sum(sizes)
        h = xn @ w[:, start:start + width]
        cuts, acc = [], 0
        for s in sizes[:-1]:
            acc += s
            cuts.append(acc)
        groups.append(jnp.split(h, cuts, axis=-1))
        start += width
    return groups


def blockify(t, nb):
    return t.reshape(t.shape[0], nb, BLOCK, *t.shape[2:]).swapaxes(0, 1)


def band_bias(table, step):
    rel = BLOCK + jnp.arange(BLOCK)[:, None] - jnp.arange(2 * BLOCK)[None, :]
    return table[t5_bucket(rel * step)].transpose(2, 0, 1)


def banded_attention(q, k, v, max_dist, bias, sink=None):
    b, L, hq, dh = q.shape
    hkv = k.shape[2]
    g = hq // hkv
    nb = L // BLOCK
    qb = q.reshape(b, nb, BLOCK, hkv, g, dh)

    def with_prev(t):
        tb = t.reshape(b, nb, BLOCK, hkv, dh)
        prev = jnp.pad(tb[:, :-1], ((0, 0), (1, 0), (0, 0), (0, 0), (0, 0)))
        return jnp.concatenate([prev, tb], axis=2)

    kb, vb = with_prev(k), with_prev(v)
    s = jnp.einsum('bnqkgd,bnskd->bnkgqs', qb, kb, preferred_element_type=jnp.float32) * dh ** -0.5
    s = s + bias.reshape(hkv, g, BLOCK, 2 * BLOCK).astype(jnp.float32)
    kj = jnp.arange(2 * BLOCK)[None, :]
    rel = BLOCK + jnp.arange(BLOCK)[:, None] - kj
    key_ok = (jnp.arange(nb)[:, None, None] * BLOCK - BLOCK + kj[None]) >= 0
    mask = (rel >= 0)[None] & (rel <= max_dist)[None] & key_ok
    s = jnp.where(mask[None, :, None, None], s, -jnp.inf)
    m = jnp.max(s, axis=-1)
    if sink is not None:
        sk = sink.astype(jnp.float32).reshape(hkv, g)[None, None, :, :, None]
        m = jnp.maximum(m, sk)
    p = jnp.exp(s - m[..., None])
    den = jnp.sum(p, axis=-1)
    if sink is not None:
        den = den + jnp.exp(sk - m)
    den_q = jnp.moveaxis(den, -1, 2)
    o = jnp.einsum('bnkgqs,bnskd->bnqkgd', p, vb.astype(jnp.float32)) / den_q[..., None]
    return (o.reshape(b, L, hq, dh), jnp.moveaxis(m, -1, 2).reshape(b, L, hq), den_q.reshape(b, L, hq))


def dsa_attention(q, k, v, iq, ik, iw, bias_table):
    b, S, h, dh = q.shape
    nb = S // BLOCK
    k_sel = min(TOPK_MAX, S // 4)
    key_pos = jnp.arange(S)

    def one_block(args):
        qb, iqb, iwb, qpos = args
        rel = jax.nn.relu(jnp.einsum('bqhd,bsd->bqhs', iqb, ik, preferred_element_type=jnp.float32))
        score = jnp.einsum('bqhs,bqh->bqs', rel, iwb.astype(jnp.float32))
        score = jnp.where(key_pos[None, None, :] <= qpos[None, :, None], score, -jnp.inf)
        _, idx = lax.top_k(score, k_sel)
        valid = idx <= qpos[None, :, None]
        ks = jax.vmap(lambda kk, ii: kk[ii])(k, idx)
        vs = jax.vmap(lambda vv, ii: vv[ii])(v, idx)
        logits = jnp.einsum('bqhd,bqkhd->bhqk', qb, ks, preferred_element_type=jnp.float32) * dh ** -0.5
        bias = bias_table[t5_bucket(qpos[None, :, None] - idx)]
        logits = logits + bias.transpose(0, 3, 1, 2).astype(jnp.float32)
        logits = jnp.where(valid[:, None], logits, -jnp.inf)
        p = jax.nn.softmax(logits, axis=-1)
        return jnp.einsum('bhqk,bqkhd->bqhd', p, vs.astype(jnp.float32))

    out = lax.map(one_block, (blockify(q, nb), blockify(iq, nb), blockify(iw, nb), key_pos.reshape(nb, BLOCK)))
    return out.swapaxes(0, 1).reshape(b, S, h, dh)


def dilated_mixture(q, k, v, rel_bias):
    b, S, _, h, dh = q.shape
    outs, ms, dens = [], [], []
    for g, (window, dil) in enumerate(B_GROUPS):
        sub_len = S // dil
        pad_len = -(-sub_len // BLOCK) * BLOCK

        def to_sub(t):
            t = t.reshape(b, sub_len, dil, h, dh).transpose(0, 2, 1, 3, 4).reshape(b * dil, sub_len, h, dh)
            return jnp.pad(t, ((0, 0), (0, pad_len - sub_len), (0, 0), (0, 0)))

        def from_sub(t):
            t = t[:, :sub_len]
            return t.reshape(b, dil, sub_len, *t.shape[2:]).swapaxes(1, 2).reshape(b, S, *t.shape[2:])

        table = rel_bias[:, A_HEADS + g * B_HEADS:A_HEADS + (g + 1) * B_HEADS]
        o, m, den = banded_attention(to_sub(q[:, :, g]), to_sub(k[:, :, g]), to_sub(v[:, :, g]),
                                     window // dil, band_bias(table, dil))
        outs.append(from_sub(o))
        ms.append(from_sub(m))
        dens.append(from_sub(den))
    m_all = jnp.stack(ms)
    w = jnp.stack(dens) * jnp.exp(m_all - jnp.max(m_all, axis=0, keepdims=True))
    w = w / jnp.sum(w, axis=0, keepdims=True)
    return sum(w[g][..., None] * outs[g] for g in range(len(B_GROUPS)))


def causal_dense_attention(q, k, v):
    b, S, h, dqk = q.shape
    nb = S // BLOCK
    key_pos = jnp.arange(S)
    vf = v.astype(jnp.float32)

    def one_block(args):
        qb, qpos = args
        s = jnp.einsum('bqhd,bshd->bhqs', qb, k, preferred_element_type=jnp.float32) * dqk ** -0.5
        s = jnp.where(key_pos[None, :] <= qpos[:, None], s, -jnp.inf)
        p = jax.nn.softmax(s, axis=-1)
        return jnp.einsum('bhqs,bshd->bqhd', p, vf)

    out = lax.map(one_block, (blockify(q, nb), key_pos.reshape(nb, BLOCK)))
    return out.swapaxes(0, 1).reshape(b, S, h, v.shape[-1])


def setup_inputs(seed: int = 0) -> dict:
    key = jax.random.key(seed)
    ks = jax.random.split(key, 16)
    f32 = jnp.float32

    def normal(k, shape, scale):
        return jax.random.normal(k, shape, f32) * scale

    def gain(k, shape):
        return 1.0 + 0.05 * jax.random.normal(k, shape, f32)

    return {
        'x': normal(ks[0], (BATCH, SEQ, D_MODEL), 1.0),
        'norm_gain': gain(ks[1], (DEPTH, D_MODEL)),
        'w_in': normal(ks[2], (DEPTH, D_MODEL, N_IN), D_MODEL ** -0.5),
        'qk_gain_a': gain(ks[3], (DEPTH, 2, HEAD_DIM)),
        'qk_gain_b': gain(ks[4], (DEPTH, 2, HEAD_DIM)),
        'qk_gain_c': gain(ks[5], (DEPTH, 2, C_NOPE + C_ROPE)),
        'qk_gain_d': gain(ks[6], (DEPTH, 2, HEAD_DIM)),
        'c_q_gain': gain(ks[7], (DEPTH, C_Q_LORA)),
        'c_kv_gain': gain(ks[8], (DEPTH, C_KV_LORA)),
        'w_q_b': normal(ks[9], (DEPTH, C_Q_LORA, C_HEADS * (C_NOPE + C_ROPE)), C_Q_LORA ** -0.5),
        'w_kv_b': normal(ks[10], (DEPTH, C_KV_LORA, C_HEADS * (C_NOPE + C_V)), C_KV_LORA ** -0.5),
        'sinks': normal(ks[11], (DEPTH, D_HEADS), 0.5),
        'rel_bias': normal(ks[12], (NUM_BUCKETS, N_BIAS_HEADS), 0.1),
        'w_branch': normal(ks[13], (DEPTH, N_BRANCH, BRANCH_WIDTH, D_MODEL), BRANCH_WIDTH ** -0.5),
        'w_out': normal(ks[14], (DEPTH, D_MODEL, D_MODEL), D_MODEL ** -0.5),
    }


def reference(x, norm_gain, w_in, qk_gain_a, qk_gain_b, qk_gain_c, qk_gain_d, c_q_gain, c_kv_gain,
              w_q_b, w_kv_b, sinks, rel_bias, w_branch, w_out):
    b, S, _ = x.shape
    pos = jnp.arange(S)
    bias_a = rel_bias[:, :A_HEADS]
    bias_d = rel_bias[:, N_BIAS_HEADS - D_HEADS:]
    n_g = len(B_GROUPS)
    for l in range(DEPTH):
        xn = rms_norm(x, norm_gain[l])
        ((a_q, a_k, a_v, a_z, a_iq, a_ik, a_iw), (b_q, b_k, b_v, b_z),
         (c_q, c_kv, c_pe, c_z), (d_q, d_k, d_v, d_z), gates) = combined_projection(xn, w_in[l])

        shp_a = (b, S, A_HEADS, HEAD_DIM)
        qa = rms_norm(a_q.reshape(shp_a), qk_gain_a[l, 0])
        ka = rms_norm(a_k.reshape(shp_a), qk_gain_a[l, 1])
        ya = dsa_attention(qa, ka, a_v.reshape(shp_a), a_iq.reshape(b, S, IDX_HEADS, IDX_DIM), a_ik, a_iw, bias_a)

        shp_b = (b, S, n_g, B_HEADS, HEAD_DIM)
        qbm = rms_norm(b_q.reshape(shp_b), qk_gain_b[l, 0])
        kbm = rms_norm(b_k.reshape(shp_b), qk_gain_b[l, 1])
        yb = dilated_mixture(qbm, kbm, b_v.reshape(shp_b), rel_bias)

        qc = (rms_norm(c_q, c_q_gain[l]) @ w_q_b[l]).reshape(b, S, C_HEADS, C_NOPE + C_ROPE)
        kv = (rms_norm(c_kv, c_kv_gain[l]) @ w_kv_b[l]).reshape(b, S, C_HEADS, C_NOPE + C_V)
        kc = jnp.concatenate([kv[..., :C_NOPE],
                              jnp.broadcast_to(c_pe[:, :, None, :], (b, S, C_HEADS, C_ROPE))], axis=-1)
        qc = rms_norm(qc, qk_gain_c[l, 0])
        kc = rms_norm(kc, qk_gain_c[l, 1])
        qc = jnp.concatenate([qc[..., :C_NOPE], rope(qc[..., C_NOPE:], pos)], axis=-1)
        kc = jnp.concatenate([kc[..., :C_NOPE], rope(kc[..., C_NOPE:], pos)], axis=-1)
        yc = causal_dense_attention(qc, kc, kv[..., C_NOPE:])

        qd = rms_norm(d_q.reshape(b, S, D_HEADS, HEAD_DIM), qk_gain_d[l, 0])
        kd = rms_norm(d_k.reshape(b, S, D_KV_HEADS, HEAD_DIM), qk_gain_d[l, 1])
        yd, _, _ = banded_attention(qd, kd, d_v.reshape(b, S, D_KV_HEADS, HEAD_DIM),
                                    D_WINDOW - 1, band_bias(bias_d, 1), sinks[l])

        ys = (ya, yb, yc, yd)
        zs = (a_z, b_z, c_z, d_z)
        merged = sum(jax.nn.sigmoid(gates[n]) *
                     ((ys[n].reshape(b, S, BRANCH_WIDTH).astype(x.dtype) * jax.nn.silu(zs[n])) @ w_branch[l, n])
                     for n in range(N_BRANCH))
        x = x + merged @ w_out[l]
    return x
```

```python
from contextlib import ExitStack
import numpy as np
import ml_dtypes
import concourse.bass as bass
import concourse.mybir as mybir
from concourse.bass_utils import run_bass_kernel_spmd

F32 = mybir.dt.float32
BF16 = mybir.dt.bfloat16
AF = mybir.ActivationFunctionType
ALU = mybir.AluOpType
AX = mybir.AxisListType

D_MODEL = 1024
EPS = 1e-6
NEG = -30000.0
BIG = 1.0e30
TG = 2048
NBIS = 28
TOPK = 256
B_DIL = (1, 4, 16)
NCOLG = 28
NPV = 1024 + 64 * 6 + 96 * 2 + 256 + 128 + 8

A0, B0_, C0, D0, G0 = 0, 2344, 7464, 8392, 9672


def _w_in_cols():
    groups = []
    ar = np.arange
    groups.append(A0 + 1536 + ar(512))
    groups.append(B0_ + 4608 + ar(512))
    groups.append(C0 + 416 + ar(512))
    groups.append(D0 + 768 + ar(512))
    for n in range(8):
        groups.append(G0 + n * 512 + ar(512))
    groups.append(A0 + ar(512))
    groups.append(A0 + 512 + ar(512))
    groups.append(A0 + 1024 + ar(512))
    for g in range(3):
        groups.append(B0_ + g * 512 + ar(512))
    for g in range(3):
        groups.append(B0_ + 1536 + g * 512 + ar(512))
    for g in range(3):
        groups.append(B0_ + 3072 + g * 512 + ar(512))
    dq = np.concatenate([D0 + h * 64 + ar(64) for h in (0, 4, 1, 5, 2, 6, 3, 7)])
    groups.append(dq)
    groups.append(np.concatenate([D0 + 512 + ar(128), D0 + 640 + ar(128), A0 + 2048 + ar(256)]))
    pad = -np.ones(88, dtype=np.int64)
    groups.append(np.concatenate([C0 + ar(256), C0 + 256 + ar(128), C0 + 384 + ar(32),
                                  A0 + 2336 + ar(8), pad]))
    ik = A0 + 2304 + ar(32)
    groups.append(np.concatenate([ik, ik, ik, ik, -np.ones(384, dtype=np.int64)]))
    cols = np.concatenate(groups)
    assert cols.shape[0] == NCOLG * 512
    return cols


def _t5_bucket(dist):
    d = np.maximum(dist, 0)
    logd = np.log(np.maximum(d, 1).astype(np.float32) / np.float32(16))
    large = 16 + (logd / np.float32(np.log(2048 / 16)) * np.float32(16)).astype(np.int32)
    return np.where(d < 16, d, np.minimum(large, 31))


class Sched:
    def __init__(self, nc, n_dma=24):
        self.nc = nc
        self.e = dict(pe=nc.tensor, act=nc.scalar, dve=nc.vector, pool=nc.gpsimd, sp=nc.sync)
        self.sem = {k: nc.alloc_semaphore("s_" + k) for k in self.e}
        self.tick = {k: 0 for k in self.e}
        self.seen = {k: {} for k in self.e}
        self.dsem = [nc.alloc_semaphore("dm%d" % i) for i in range(n_dma)]
        self.dcnt = [0] * n_dma
        self.dnext = 0
        self.W = {}
        self.R = {}
        self.nins = 0

    def _wait(self, eng, evs):
        for sid, (sem, val) in evs.items():
            if self.seen[eng].get(sid, 0) < val:
                self.e[eng].wait_ge(sem, val)
                self.seen[eng][sid] = val
                self.nins += 1

    def _deps(self, eng, r, w):
        evs = {}

        def add(d):
            for sid, (sem, val) in d.items():
                if sid not in evs or evs[sid][1] < val:
                    evs[sid] = (sem, val)
        for k in r:
            add(self.W.get(k, {}))
            if isinstance(k, str) and k[0] == 'P' and (k[1:].isdigit() or k[1] == 'B'):
                add({sid: v for sid, v in self.R.get(k, {}).items() if sid != eng})
        for k in w:
            add(self.W.get(k, {}))
            add(self.R.get(k, {}))
        if eng == 'pe':
            evs.pop('pe', None)
        return evs

    def _commit(self, ev, r, w):
        sid, sem, val = ev
        for k in w:
            self.W[k] = {sid: (sem, val)}
            self.R[k] = {}
        for k in r:
            d = self.R.setdefault(k, {})
            if sid not in d or d[sid][1] < val:
                d[sid] = (sem, val)

    def op(self, eng, fn, r=(), w=(), sig=True):
        self._wait(eng, self._deps(eng, r, w))
        ins = fn()
        self.nins += 1
        if sig:
            self.tick[eng] += 1
            ins.then_inc(self.sem[eng], 1)
            val = self.tick[eng]
        else:
            val = self.tick[eng] + 1
        self._commit((eng, self.sem[eng], val), r, w)

    def dma(self, out, in_, r=(), w=(), q='sp'):
        i = self.dnext
        self.dnext = (i + 1) % len(self.dsem)
        evs = self._deps(q, r, w)
        sid = 'd%d' % i
        if self.dcnt[i] > 0:
            v = 16 * self.dcnt[i]
            if sid not in evs or evs[sid][1] < v:
                evs[sid] = (self.dsem[i], v)
        self._wait(q, evs)
        self.e[q].dma_start(out=out, in_=in_).then_inc(self.dsem[i], 16)
        self.nins += 1
        self.dcnt[i] += 1
        self._commit((sid, self.dsem[i], 16 * self.dcnt[i]), r, w)

    def barrier(self):
        evs = {}
        for k in self.e:
            if self.tick[k] > 0:
                evs[k] = (self.sem[k], self.tick[k])
        for i, s in enumerate(self.dsem):
            if self.dcnt[i] > 0:
                evs['d%d' % i] = (s, 16 * self.dcnt[i])
        for k in self.e:
            self._wait(k, dict(evs))
        self.W = {}
        self.R = {}


class Prog:
    def __init__(self, S, L, dbg=False):
        self.S, self.L, self.dbg = S, L, dbg
        self.NT = S // 128
        self.NG = S // TG
        assert S % TG == 0
        nc = self.nc = bass.Bass("TRN2", target_bir_lowering=False)
        self.sc = Sched(nc)

        def din(name, shape, dt=F32):
            return nc.dram_tensor(name, list(shape), dt, kind="ExternalInput").ap()

        def dscr(name, shape, dt=BF16):
            kind = "ExternalOutput" if dbg else "Internal"
            return nc.dram_tensor(name, list(shape), dt, kind=kind).ap()
        self.x_in = din("x", [S, 1024])
        self.w_in = din("w_in", [L, 1024, NCOLG * 512])
        self.wqb = din("wqb", [L, 256, 768])
        self.wkvb = din("wkvb", [L, 128, 1024])
        self.wbr = din("wbr", [L, 4, 512, 1024])
        self.wout = din("wout", [L, 1024, 1024])
        self.pv = din("pv", [L, NPV])
        self.cst = din("cst", [128, 128 * 3 + 32])
        self.rope = din("rope", [S, 32])
        self.biasA = din("biasA", [8, 128, 13 * 128])
        self.b31 = din("b31", [128, 8])
        self.biasB = din("biasB", [128, 3 * 8 * 2 * 128])
        self.biasD = din("biasD", [128, 8 * 2 * 128])
        self.y_out = nc.dram_tensor("y", [S, 1024], F32, kind="ExternalOutput").ap()
        self.X = [dscr("X0", [S, 1024], F32), dscr("X1", [S, 1024], F32)]
        self.ZGT = dscr("ZGT", [6144, S])
        self.QT_A = dscr("QT_A", [512, S])
        self.KT_A = dscr("KT_A", [512, S])
        self.V_A = dscr("V_A", [S, 512])
        self.QT_B = [dscr("QT_B%d" % g, [512, S]) for g in range(3)]
        self.KT_B = [dscr("KT_B%d" % g, [512, S]) for g in range(3)]
        self.V_B = [dscr("V_B%d" % g, [S, 512]) for g in range(3)]
        self.QT_C = dscr("QT_C", [8, 96, S])
        self.KT_C = dscr("KT_C", [8, 96, S])
        self.V_C = dscr("V_C", [S, 512])
        self.QT_D = dscr("QT_D", [512, S])
        self.KT_D = dscr("KT_D", [128, S])
        self.V_D = dscr("V_D", [S, 128])
        self.IQT = dscr("IQT", [256, S])
        self.IKT = dscr("IKT", [128, S])
        self.IW = dscr("IW", [S, 8], F32)
        self.YT = [dscr("YT%d" % m, [512, S]) for m in range(4)]
        self.THR = dscr("THR", [S, 1], F32) if dbg else None
        self.P = [nc.alloc_psum_tensor("P%d" % i, [128, 512], F32) for i in range(6)]
        self.PB = [nc.alloc_psum_tensor("PB%d" % i, [128, 1024], BF16) for i in range(2)]
        self.cst_f = nc.alloc_sbuf_tensor("cst_f", [128, 128 * 3 + 32], F32)
        self.ident = nc.alloc_sbuf_tensor("ident", [128, 128], BF16)
        self.caus_kq = nc.alloc_sbuf_tensor("caus_kq", [128, 128], BF16)
        self.ones_f = nc.alloc_sbuf_tensor("ones_f", [128, 64], F32)
        sc = self.sc
        sc.dma(self.cst_f[:], self.cst[:, :], w=['cst_f'])
        sc.op('dve', lambda: nc.vector.tensor_copy(out=self.ident[:], in_=self.cst_f[:, 0:128]),
              r=['cst_f'], w=['ident'])
        sc.op('dve', lambda: nc.vector.tensor_copy(out=self.caus_kq[:], in_=self.cst_f[:, 256:384]),
              r=['cst_f'], w=['caus_kq'])
        sc.op('dve', lambda: nc.vector.memset(self.ones_f[:], 1.0), w=['ones_f'])
        self.causneg = self.cst_f[:, 128:256]
        self.pow2 = self.cst_f[:, 384:416]

    def headnorm(self, T, src, srckey, nh, hd, gain, out, outkey):
        nc, sc = self.nc, self.sc
        n = nh * hd
        sq, ss, ss2, rs, tn = T['sq'], T['ss'], T['ss2'], T['rs'], T['tn']
        sc.op('act', lambda: nc.scalar.activation(out=sq[:, :n], in_=src, func=AF.Square),
              r=[srckey], w=['sq'])
        sc.op('dve', lambda: nc.vector.tensor_reduce(
            out=ss[:, :nh], in_=sq[:, :n].rearrange("p (h d) -> p h d", d=hd), axis=AX.X, op=ALU.add),
            r=['sq'], w=['ss'])
        sc.op('act', lambda: nc.scalar.activation(out=ss2[:, :nh], in_=ss[:, :nh], func=AF.Sqrt,
                                                   scale=1.0 / hd, bias=T['eps'][:, 0:1]),
              r=['ss'], w=['ss2'])
        sc.op('dve', lambda: nc.vector.reciprocal(out=rs[:, :nh], in_=ss2[:, :nh]), r=['ss2'], w=['rs'])
        sc.op('dve', lambda: nc.vector.tensor_tensor(
            out=tn[:, :n].rearrange("p (h d) -> p h d", d=hd),
            in0=src.rearrange("p (h d) -> p h d", d=hd),
            in1=rs[:, :nh].unsqueeze(2).broadcast_to([128, nh, hd]), op=ALU.mult),
            r=[srckey, 'rs'], w=['tn'])
        sc.op('dve', lambda: nc.vector.tensor_tensor(
            out=out.rearrange("p (h d) -> p h d", d=hd),
            in0=tn[:, :n].rearrange("p (h d) -> p h d", d=hd),
            in1=gain.unsqueeze(1).broadcast_to([128, nh, hd]), op=ALU.mult),
            r=['tn', 'pvb'], w=[outkey])

    def phase1(self, l, Xsrc):
        nc, sc, S = self.nc, self.sc, self.S
        P, PB = self.P, self.PB
        with ExitStack() as es:
            def sb(name, shape, dt):
                return es.enter_context(nc.sbuf_tensor("p1_%d_" % l + name, list(shape), dt))
            pvb = sb("pvb", [128, NPV], F32)
            gs = sb("gs", [128, 64 * 3 + 96], F32)
            T = dict(sq=sb("sq", [128, 1024], F32), ss=sb("ss", [128, 8], F32), ss2=sb("ss2", [128, 8], F32),
                     rs=sb("rs", [128, 8], F32), tn=sb("tn", [128, 768], F32), eps=sb("eps", [128, 1], F32))
            xts = [sb("xt%d" % i, [128, 1024], F32) for i in range(2)]
            xn = [sb("xn%d" % i, [128, 1024], BF16) for i in range(2)]
            xnT = sb("xnT", [128, 8, TG], BF16)
            wst = [sb("wst%d" % i, [128, 8, 512], F32) for i in range(2)]
            wbf = [sb("wbf%d" % i, [128, 8, 512], BF16) for i in range(2)]
            fmst = [sb("fmst%d" % i, [128, TG], BF16) for i in range(2)]
            stg = [sb("stg%d" % i, [128, 4, TG], BF16) for i in range(1)]
            ob = [sb("ob%d" % i, [128, 512], BF16) for i in range(2)]
            vb = [sb("vb%d" % i, [128, 512], BF16) for i in range(2)]
            wqb_f = sb("wqb_f", [128, 2, 768], F32)
            wqb_b = sb("wqb_b", [128, 2, 768], BF16)
            wkvb_f = sb("wkvb_f", [128, 1024], F32)
            wkvb_b = sb("wkvb_b", [128, 1024], BF16)
            cqn = sb("cqn", [128, 256], BF16)
            ckvn = sb("ckvn", [128, 128], BF16)
            cqT = sb("cqT", [128, 2, 128], BF16)
            ckvT = sb("ckvT", [128, 128], BF16)
            qcf = sb("qcf", [128, 768], F32)
            qcb = sb("qcb", [128, 768], BF16)
            kcb = sb("kcb", [128, 768], BF16)
            krp = sb("krp", [128, 32], F32)
            krr = sb("krr", [128, 32], F32)
            rtmp = sb("rtmp", [128, 8, 16], F32)
            rtmp2 = sb("rtmp2", [128, 8, 16], F32)
            cs = sb("cs", [128, 32], F32)
            iwt = sb("iwt", [128, 8], F32)
            ssr = sb("ssr", [128, 1], F32)
            cstg_q = sb("cstg_q", [96, 8, 512], BF16)
            cstg_k = sb("cstg_k", [96, 8, 512], BF16)
            stg_dk = sb("stg_dk", [128, TG], BF16)
            stg_iq = sb("stg_iq", [128, 2, TG], BF16)
            stg_ik = sb("stg_ik", [128, TG], BF16)
            dkb = sb("dkb", [128, 128], BF16)
            iqb = sb("iqb", [128, 256], BF16)
            ikb = sb("ikb", [128, 128], BF16)

            sc.op('dve', lambda: nc.vector.memset(T['eps'][:], EPS), w=['eps'])
            sc.dma(pvb[:], self.pv[l].partition_broadcast(128), w=['pvb'])
            o = 1024
            g_aq, g_ak, g_bq, g_bk = (pvb[:, o + 64 * i: o + 64 * (i + 1)] for i in range(4))
            o += 256
            g_cq, g_ck = pvb[:, o:o + 96], pvb[:, o + 96:o + 192]
            o += 192
            g_dq, g_dk = pvb[:, o:o + 64], pvb[:, o + 64:o + 128]
            o += 128
            g_cql, g_ckvl = pvb[:, o:o + 256], pvb[:, o + 256:o + 384]
            for i, (g, s_) in enumerate(((g_aq, 0.125), (g_bq, 0.125), (g_dq, 0.125))):
                sc.op('dve', lambda g=g, s_=s_, i=i: nc.vector.tensor_scalar(
                    out=gs[:, 64 * i:64 * (i + 1)], in0=g, scalar1=s_, scalar2=None, op0=ALU.mult),
                    r=['pvb'], w=['pvb'])
            sc.op('dve', lambda: nc.vector.tensor_scalar(
                out=gs[:, 192:288], in0=g_cq, scalar1=96 ** -0.5, scalar2=None, op0=ALU.mult),
                r=['pvb'], w=['pvb'])
            gs_aq, gs_bq, gs_dq, gs_cq = gs[:, 0:64], gs[:, 64:128], gs[:, 128:192], gs[:, 192:288]
            sc.dma(wqb_f[:], self.wqb[l].rearrange("(c p) n -> p c n", p=128), w=['wqb_f'])
            sc.op('pool', lambda: nc.gpsimd.tensor_copy(out=wqb_b[:], in_=wqb_f[:]), r=['wqb_f'], w=['wqb_b'])
            sc.dma(wkvb_f[:], self.wkvb[l], w=['wkvb_f'])
            sc.op('pool', lambda: nc.gpsimd.tensor_copy(out=wkvb_b[:], in_=wkvb_f[:]), r=['wkvb_f'], w=['wkvb_b'])

            wv = self.w_in[l].rearrange("(c p) n -> p c n", p=128)
            wcnt = [0]

            def load_w(grp, ncols=512):
                i = wcnt[0] % 2
                wcnt[0] += 1
                sc.dma(wst[i][:, :, :ncols], wv[:, :, grp * 512: grp * 512 + ncols], w=['wst%d' % i])
                sc.op('pool', lambda: nc.gpsimd.tensor_copy(out=wbf[i][:, :, :ncols], in_=wst[i][:, :, :ncols]),
                      r=['wst%d' % i], w=['wbf%d' % i])
                return i

            GW = [512] * 27 + [128]
            for tg in range(self.NG):
                t0 = tg * TG
                for t in range(16):
                    xt, xnb = xts[t % 2], xn[t % 2]
                    kx, kn = 'xt%d' % (t % 2), 'xn%d' % (t % 2)
                    sc.dma(xt[:], Xsrc[t0 + t * 128: t0 + (t + 1) * 128, :], w=[kx])
                    sc.op('act', lambda xt=xt: nc.scalar.activation(
                        out=T['sq'][:], in_=xt[:], func=AF.Square, accum_out=T['ss'][:, 0:1]),
                        r=[kx], w=['sq', 'ss'])
                    sc.op('act', lambda: nc.scalar.activation(
                        out=T['ss2'][:, 0:1], in_=T['ss'][:, 0:1], func=AF.Sqrt, scale=1.0 / 1024,
                        bias=T['eps'][:, 0:1]), r=['ss', 'eps'], w=['ss2'])
                    sc.op('dve', lambda: nc.vector.reciprocal(out=T['rs'][:, 0:1], in_=T['ss2'][:, 0:1]),
                          r=['ss2'], w=['rs'])
                    sc.op('dve', lambda xt=xt, xnb=xnb: nc.vector.scalar_tensor_tensor(
                        out=xnb[:], in0=xt[:], scalar=T['rs'][:, 0:1], in1=pvb[:, 0:1024],
                        op0=ALU.mult, op1=ALU.mult), r=[kx, 'rs', 'pvb'], w=[kn])
                    pb = PB[t % 2]
                    kp = 'PB%d' % (t % 2)
                    for c in range(8):
                        sc.op('pe', lambda c=c, pb=pb, xnb=xnb: nc.tensor.transpose(
                            out=pb[:, c * 128:(c + 1) * 128], in_=xnb[:, c * 128:(c + 1) * 128],
                            identity=self.ident[:]), r=[kn, 'ident'], w=[kp], sig=(c == 7))
                    sc.op('act', lambda pb=pb, t=t: nc.scalar.copy(
                        out=xnT[:, :, t * 128:(t + 1) * 128],
                        in_=pb[:, :].rearrange("p (c t) -> p c t", c=8)), r=[kp], w=['xnT'])
                cur = load_w(0)
                for grp in range(NCOLG):
                    nxt = load_w(grp + 1, GW[grp + 1]) if grp + 1 < NCOLG else None
                    wb = wbf[cur]
                    kw = 'wbf%d' % cur
                    if grp < 12:
                        func = AF.Silu if grp < 4 else AF.Sigmoid
                        for sub in range(4):
                            st = fmst[sub % 2]
                            ks = 'fmst%d' % (sub % 2)
                            for tq in range(4):
                                pi = (sub * 4 + tq) % 2
                                ps = P[pi]
                                for c in range(8):
                                    sc.op('pe', lambda c=c, ps=ps, wb=wb, sub=sub, tq=tq: nc.tensor.matmul(
                                        ps[:, :], lhsT=wb[:, c, sub * 128:(sub + 1) * 128],
                                        rhs=xnT[:, c, tq * 512:(tq + 1) * 512], start=(c == 0), stop=(c == 7)),
                                        r=[kw, 'xnT'], w=['P%d' % pi], sig=(c == 7))
                                sc.op('act', lambda ps=ps, st=st, tq=tq, func=func: nc.scalar.activation(
                                    out=st[:, tq * 512:(tq + 1) * 512], in_=ps[:, :], func=func),
                                    r=['P%d' % pi], w=[ks])
                            row = (grp * 4 + sub) * 128
                            sc.dma(self.ZGT[row:row + 128, t0:t0 + TG], st[:], r=[ks], w=[('ZGT', grp, sub, tg)])
                    else:
                        for t in range(16):
                            pi = 2 + (t % 2)
                            ps = P[pi]
                            kps = 'P%d' % pi
                            ncol = GW[grp]
                            for c in range(8):
                                sc.op('pe', lambda c=c, ps=ps, wb=wb, t=t, ncol=ncol: nc.tensor.matmul(
                                    ps[:, :ncol], lhsT=xnT[:, c, t * 128:(t + 1) * 128], rhs=wb[:, c, :ncol],
                                    start=(c == 0), stop=(c == 7)), r=[kw, 'xnT'], w=[kps], sig=(c == 7))
                            self._p1_post(l, grp, tg, t, ps, kps, locals())
                    cur = nxt

    def _evac_T(self, pb, kp, dst, kdst, t, d, npair=4, eng='act'):
        nc, sc = self.nc, self.sc
        nj = 128 // d
        if npair > 1:
            src = pb[:, 0:npair * 128].rearrange("q (p jj r) -> q p r jj", p=npair, r=d)
            dd = dst.rearrange("q p (r m) -> q p r m", r=d)[:, :, :, nj * t: nj * t + nj]
        else:
            src = pb[:, 0:128].rearrange("q (jj r) -> q r jj", r=d)
            dd = dst.rearrange("q (r m) -> q r m", r=d)[:, :, nj * t: nj * t + nj]
        if eng == 'act':
            sc.op('act', lambda: nc.scalar.copy(out=dd, in_=src), r=[kp], w=[kdst])
        else:
            sc.op('dve', lambda: nc.vector.tensor_copy(out=dd, in_=src), r=[kp], w=[kdst])

    def _p1_post(self, l, grp, tg, t, ps, kps, L_):
        nc, sc, S = self.nc, self.sc, self.S
        PB = self.PB
        T = L_['T']
        t0 = tg * TG
        tok0 = t0 + t * 128
        ident = self.ident
        qk_groups = {12: ('aq', 1), 13: ('ak', 1), 15: ('bq', 1), 16: ('bq', 4), 17: ('bq', 16),
                     18: ('bk', 1), 19: ('bk', 4), 20: ('bk', 16), 24: ('dq', 1)}
        v_groups = {14: (self.V_A, 1), 21: (self.V_B[0], 1), 22: (self.V_B[1], 4), 23: (self.V_B[2], 16)}
        if grp in qk_groups:
            kind, d = qk_groups[grp]
            gain = dict(aq=L_['gs_aq'], ak=L_['g_ak'], bq=L_['gs_bq'], bk=L_['g_bk'], dq=L_['gs_dq'])[kind]
            dst = {12: self.QT_A, 13: self.KT_A, 15: self.QT_B[0], 16: self.QT_B[1], 17: self.QT_B[2],
                   18: self.KT_B[0], 19: self.KT_B[1], 20: self.KT_B[2], 24: self.QT_D}[grp]
            ob = L_['ob'][t % 2]
            ko = 'ob%d' % (t % 2)
            self.headnorm(T, ps[:, :], kps, 8, 64, gain, ob[:, :], ko)
            pb = PB[t % 2]
            kp = 'PB%d' % (t % 2)
            for p in range(4):
                sc.op('pe', lambda p=p: nc.tensor.transpose(
                    out=pb[:, p * 128:(p + 1) * 128], in_=ob[:, p * 128:(p + 1) * 128], identity=ident[:]),
                    r=[ko, 'ident'], w=[kp], sig=(p == 3))
            si = 0
            st = L_['stg'][si]
            self._evac_T(pb, kp, st[:], 'stg%d' % si, t, d)
            if t == 15:
                for p_ in range(4):
                    sc.dma(dst[p_ * 128:(p_ + 1) * 128, t0:t0 + TG], st[:, p_, :], r=['stg%d' % si], w=[(grp, tg, p_)])
        elif grp in v_groups:
            dst, d = v_groups[grp]
            vb = L_['vb'][t % 2]
            kv = 'vb%d' % (t % 2)
            sc.op('act', lambda: nc.scalar.copy(out=vb[:, :], in_=ps[:, :]), r=[kps], w=[kv])
            nj = 128 // d
            if d == 1:
                sc.dma(dst[tok0:tok0 + 128, :], vb[:, :], r=[kv], w=[(grp, tg, t)])
            else:
                dv = dst[t0:t0 + TG, :].rearrange("(r m) c -> m r c", r=d)[nj * t: nj * t + nj, :, :]
                sc.dma(dv, vb[:, :], r=[kv], w=[(grp, tg, t)])
        elif grp == 25:
            dkb, iqb, vb = L_['dkb'], L_['iqb'], L_['vb'][t % 2]
            kv = 'vb%d' % (t % 2)
            self.headnorm(T, ps[:, 0:128], kps, 2, 64, L_['g_dk'], dkb[:, :], 'dkb')
            sc.op('act', lambda: nc.scalar.copy(out=vb[:, 0:128], in_=ps[:, 128:256]), r=[kps], w=[kv])
            sc.dma(self.V_D[tok0:tok0 + 128, :], vb[:, 0:128], r=[kv], w=[('vd', tg, t)])
            sc.op('act', lambda: nc.scalar.copy(out=iqb[:, :], in_=ps[:, 256:512]), r=[kps], w=['iqb'])
            pb = PB[t % 2]
            kp = 'PB%d' % (t % 2)
            sc.op('pe', lambda: nc.tensor.transpose(out=pb[:, 0:128], in_=dkb[:, :], identity=ident[:]),
                  r=['dkb', 'ident'], w=[kp], sig=False)
            for p in range(2):
                sc.op('pe', lambda p=p: nc.tensor.transpose(
                    out=pb[:, (p + 1) * 128:(p + 2) * 128], in_=iqb[:, p * 128:(p + 1) * 128], identity=ident[:]),
                    r=['iqb', 'ident'], w=[kp], sig=(p == 1))
            sc.op('act', lambda: nc.scalar.copy(out=L_['stg_dk'][:, t * 128:(t + 1) * 128], in_=pb[:, 0:128]),
                  r=[kp], w=['stg_dk'])
            sc.op('act', lambda: nc.scalar.copy(
                out=L_['stg_iq'][:, :, t * 128:(t + 1) * 128],
                in_=pb[:, 128:384].rearrange("q (p t) -> q p t", p=2)), r=[kp], w=['stg_iq'])
            if t == 15:
                sc.dma(self.KT_D[:, t0:t0 + TG], L_['stg_dk'][:], r=['stg_dk'], w=[('ktd', tg)])
                for p_ in range(2):
                    sc.dma(self.IQT[p_ * 128:(p_ + 1) * 128, t0:t0 + TG], L_['stg_iq'][:, p_, :],
                           r=['stg_iq'], w=[('iqt', tg, p_)])
        elif grp == 27:
            ikb = L_['ikb']
            sc.op('act', lambda: nc.scalar.copy(out=ikb[:, :], in_=ps[:, 0:128]), r=[kps], w=['ikb'])
            pb = PB[t % 2]
            kp = 'PB%d' % (t % 2)
            sc.op('pe', lambda: nc.tensor.transpose(out=pb[:, 0:128], in_=ikb[:, :], identity=ident[:]),
                  r=['ikb', 'ident'], w=[kp])
            sc.op('act', lambda: nc.scalar.copy(out=L_['stg_ik'][:, t * 128:(t + 1) * 128], in_=pb[:, 0:128]),
                  r=[kp], w=['stg_ik'])
            if t == 15:
                sc.dma(self.IKT[:, t0:t0 + TG], L_['stg_ik'][:], r=['stg_ik'], w=[('ikt', tg)])
        elif grp == 26:
            self._p1_mla(l, tg, t, ps, kps, L_)

    def _p1_mla(self, l, tg, t, ps, kps, L_):
        nc, sc = self.nc, self.sc
        P, PB = self.P, self.PB
        T = L_['T']
        ident = self.ident
        t0 = tg * TG
        tok0 = t0 + t * 128
        cqn, ckvn, cqT, ckvT = L_['cqn'], L_['ckvn'], L_['cqT'], L_['ckvT']
        qcf, qcb, kcb, krp, krr = L_['qcf'], L_['qcb'], L_['kcb'], L_['krp'], L_['krr']
        rtmp, rtmp2, cs, iwt, ssr = L_['rtmp'], L_['rtmp2'], L_['cs'], L_['iwt'], L_['ssr']
        sc.op('act', lambda: nc.scalar.copy(out=iwt[:, :], in_=ps[:, 416:424]), r=[kps], w=['iwt'])
        sc.dma(self.IW[tok0:tok0 + 128, :], iwt[:, :], r=['iwt'], w=[('iw', tg, t)])
        sc.dma(cs[:, :], self.rope[tok0:tok0 + 128, :], w=['cs'])
        self.headnorm(T, ps[:, 0:256], kps, 1, 256, L_['g_cql'], cqn[:, :], 'cqn')
        self.headnorm(T, ps[:, 256:384], kps, 1, 128, L_['g_ckvl'], ckvn[:, :], 'ckvn')
        sc.op('act', lambda: nc.scalar.activation(out=krp[:, :], in_=ps[:, 384:416], func=AF.Square,
                                                   accum_out=ssr[:, 0:1]), r=[kps], w=['krp', 'ssr'])
        sc.op('dve', lambda: nc.vector.tensor_tensor(out=krp[:, :], in0=ps[:, 384:416], in1=L_['g_ck'][:, 64:96],
                                                     op=ALU.mult), r=[kps, 'pvb', 'krp'], w=['krp'])
        pb = PB[t % 2]
        kp = 'PB%d' % (t % 2)
        for p in range(2):
            sc.op('pe', lambda p=p: nc.tensor.transpose(out=pb[:, p * 128:(p + 1) * 128],
                                                        in_=cqn[:, p * 128:(p + 1) * 128], identity=ident[:]),
                  r=['cqn', 'ident'], w=[kp], sig=False)
        sc.op('pe', lambda: nc.tensor.transpose(out=pb[:, 256:384], in_=ckvn[:, :], identity=ident[:]),
              r=['ckvn', 'ident'], w=[kp])
        sc.op('act', lambda: nc.scalar.copy(out=cqT[:, :, :], in_=pb[:, 0:256].rearrange("q (p t) -> q p t", p=2)),
              r=[kp], w=['cqT'])
        sc.op('act', lambda: nc.scalar.copy(out=ckvT[:, :], in_=pb[:, 256:384]), r=[kp], w=['ckvT'])
        for half, (pp, c0, c1) in enumerate(((P[4], 0, 512), (P[5], 512, 768))):
            for c in range(2):
                sc.op('pe', lambda c=c, pp=pp, c0=c0, c1=c1: nc.tensor.matmul(
                    pp[:, 0:c1 - c0], lhsT=cqT[:, c, :], rhs=L_['wqb_b'][:, c, c0:c1], start=(c == 0), stop=(c == 1)),
                    r=['cqT', 'wqb_b'], w=['P%d' % (4 + half)], sig=(c == 1))
            sc.op('act', lambda pp=pp, c0=c0, c1=c1: nc.scalar.copy(out=qcf[:, c0:c1], in_=pp[:, 0:c1 - c0]),
                  r=['P%d' % (4 + half)], w=['qcf'])
        self.headnorm(T, qcf[:, :], 'qcf', 8, 96, L_['gs_cq'], qcf[:, :], 'qcf')
        self._rope(qcf, 'qcf', qcb, 'qcb', cs, rtmp, rtmp2)
        for half in range(2):
            sc.op('pe', lambda half=half: nc.tensor.matmul(
                P[4 + half][:, :], lhsT=ckvT[:, :], rhs=L_['wkvb_b'][:, half * 512:(half + 1) * 512],
                start=True, stop=True), r=['ckvT', 'wkvb_b'], w=['P%d' % (4 + half)])
        vb = L_['vb'][t % 2]
        kv = 'vb%d' % (t % 2)
        sc.op('act', lambda: nc.scalar.copy(out=vb[:, :], in_=P[5][:, :]), r=['P5'], w=[kv])
        sc.dma(self.V_C[tok0:tok0 + 128, :], vb[:, :], r=[kv], w=[('vc', tg, t)])
        sq, ss, ss2, rs = T['sq'], T['ss'], T['ss2'], T['rs']
        sc.op('act', lambda: nc.scalar.activation(out=sq[:, :512], in_=P[4][:, :], func=AF.Square),
              r=['P4'], w=['sq'])
        sc.op('dve', lambda: nc.vector.tensor_reduce(out=ss[:, :8], in_=sq[:, :512].rearrange("p (h d) -> p h d", d=64),
                                                     axis=AX.X, op=ALU.add), r=['sq'], w=['ss'])
        sc.op('dve', lambda: nc.vector.tensor_scalar(out=ss[:, :8], in0=ss[:, :8], scalar1=ssr[:, 0:1], scalar2=None,
                                                     op0=ALU.add), r=['ss', 'ssr'], w=['ss'])
        sc.op('act', lambda: nc.scalar.activation(out=ss2[:, :8], in_=ss[:, :8], func=AF.Sqrt, scale=1.0 / 96,
                                                   bias=T['eps'][:, 0:1]), r=['ss', 'eps'], w=['ss2'])
        sc.op('dve', lambda: nc.vector.reciprocal(out=rs[:, :8], in_=ss2[:, :8]), r=['ss2'], w=['rs'])
        kv3 = qcf[:, :].rearrange("p (h d) -> p h d", d=96)
        sc.op('dve', lambda: nc.vector.tensor_tensor(
            out=kv3[:, :, 0:64], in0=P[4][:, :].rearrange("p (h d) -> p h d", d=64),
            in1=rs[:, :8].unsqueeze(2).broadcast_to([128, 8, 64]), op=ALU.mult), r=['P4', 'rs', 'qcb'], w=['qcf'])
        sc.op('dve', lambda: nc.vector.tensor_tensor(
            out=kv3[:, :, 0:64], in0=kv3[:, :, 0:64],
            in1=L_['g_ck'][:, 0:64].unsqueeze(1).broadcast_to([128, 8, 64]), op=ALU.mult),
            r=['qcf', 'pvb'], w=['qcf'])
        sc.op('dve', lambda: nc.vector.tensor_tensor(
            out=kv3[:, :, 64:96], in0=krp[:, :].unsqueeze(1).broadcast_to([128, 8, 32]),
            in1=rs[:, :8].unsqueeze(2).broadcast_to([128, 8, 32]), op=ALU.mult), r=['krp', 'rs', 'qcf'], w=['qcf'])
        self._rope(qcf, 'qcf', kcb, 'kcb', cs, rtmp, rtmp2)
        for which, (src, ksrc, stg_, kst, dst) in enumerate(((qcb, 'qcb', L_['cstg_q'], 'cstg_q', self.QT_C),
                                                            (kcb, 'kcb', L_['cstg_k'], 'cstg_k', self.KT_C))):
            pb = PB[which]
            kp = 'PB%d' % which
            for h in range(8):
                sc.op('pe', lambda h=h, pb=pb, src=src: nc.tensor.transpose(
                    out=pb[0:96, h * 128:(h + 1) * 128], in_=src[:, h * 96:(h + 1) * 96], identity=ident[:]),
                    r=[ksrc, 'ident'], w=[kp], sig=(h == 7))
            tt = t % 4
            sc.op('act', lambda pb=pb, stg_=stg_, tt=tt: nc.scalar.copy(
                out=stg_[:, :, tt * 128:(tt + 1) * 128], in_=pb[0:96, :].rearrange("q (h t) -> q h t", h=8)),
                r=[kp], w=[kst])
            if tt == 3:
                c0 = tok0 - 384
                for h_ in range(8):
                    sc.dma(dst[h_][:, c0:c0 + 512], stg_[:, h_, :], r=[kst], w=[(kst, tg, t, h_)])

    def _rope(self, src, ksrc, dst, kdst, cs, rtmp, rtmp2):
        nc, sc = self.nc, self.sc
        s3 = src[:, :].rearrange("p (h d) -> p h d", d=96)
        d3 = dst[:, :].rearrange("p (h d) -> p h d", d=96)
        cosb = cs[:, 0:16].unsqueeze(1).broadcast_to([128, 8, 16])
        sinb = cs[:, 16:32].unsqueeze(1).broadcast_to([128, 8, 16])
        x1, x2 = s3[:, :, 64:80], s3[:, :, 80:96]
        tt = nc.vector.tensor_tensor
        sc.op('act', lambda: nc.scalar.copy(out=d3[:, :, 0:64], in_=s3[:, :, 0:64]), r=[ksrc], w=[kdst])
        sc.op('dve', lambda: tt(out=rtmp[:], in0=x1, in1=cosb, op=ALU.mult), r=[ksrc, 'cs'], w=['rtmp'])
        sc.op('dve', lambda: tt(out=rtmp2[:], in0=x2, in1=sinb, op=ALU.mult), r=[ksrc, 'cs'], w=['rtmp2'])
        sc.op('dve', lambda: tt(out=d3[:, :, 64:80], in0=rtmp[:], in1=rtmp2[:], op=ALU.subtract),
              r=['rtmp', 'rtmp2', kdst], w=[kdst])
        sc.op('dve', lambda: tt(out=rtmp[:], in0=x2, in1=cosb, op=ALU.mult), r=[ksrc, 'cs', kdst], w=['rtmp'])
        sc.op('dve', lambda: tt(out=rtmp2[:], in0=x1, in1=sinb, op=ALU.mult), r=[ksrc, 'cs', kdst], w=['rtmp2'])
        sc.op('dve', lambda: tt(out=d3[:, :, 80:96], in0=rtmp[:], in1=rtmp2[:], op=ALU.add),
              r=['rtmp', 'rtmp2', kdst], w=[kdst])

    def wload(self, dst, src, ncols, key, step=2048):
        for c0 in range(0, ncols, step):
            c1 = min(ncols, c0 + step)
            self.sc.dma(dst[:, c0:c1], src[:, c0:c1], w=[key])

    def vload(self, dst_of, src_of, nblk, key, step=16):
        for b0 in range(0, nblk, step):
            b1 = min(nblk, b0 + step)
            self.sc.dma(dst_of(b0, b1), src_of(b0, b1), w=[key])

    def _alloc(self, es, prefix):
        nc = self.nc

        def sb(name, shape, dt):
            return es.enter_context(nc.sbuf_tensor(prefix + name, list(shape), dt))
        return sb

    def _fin_bufs(self, sb):
        return dict(o=sb("fo", [65, 512], F32), rd=sb("frd", [65, 512], F32), yb=[sb("fyb%d" % i, [64, 512], BF16) for i in range(2)],
                    cnt=[0])

    def finish_head(self, F, src, ksrc, m, row0, q0, n=512, sink=None):
        nc, sc = self.nc, self.sc
        o, rd = F['o'], F['rd']
        i = F['cnt'][0] % 2
        F['cnt'][0] += 1
        yb = F['yb'][i]
        ky = 'fyb%d' % i
        sc.op('act', lambda: nc.scalar.copy(out=o[0:65, :n], in_=src), r=[ksrc], w=['fo'])
        if sink is not None:
            sc.op('dve', lambda: nc.vector.tensor_scalar(out=rd[64:65, :n], in0=o[64:65, :n], scalar1=sink, scalar2=None,
                                                         op0=ALU.add), r=['fo', 'esk'], w=['frd'])
            sc.op('dve', lambda: nc.vector.reciprocal(out=rd[64:65, :n], in_=rd[64:65, :n]), r=['frd'], w=['frd'])
        else:
            sc.op('dve', lambda: nc.vector.reciprocal(out=rd[64:65, :n], in_=o[64:65, :n]), r=['fo'], w=['frd'])
        pb = self.P[0]
        sc.op('pe', lambda: nc.tensor.matmul(pb[0:64, :n], lhsT=self.ones_f[64:65, 0:64], rhs=rd[64:65, :n],
                                             start=True, stop=True), r=['frd', 'ones_f'], w=['P0'])
        sc.op('dve', lambda: nc.vector.tensor_tensor(out=yb[:, :n], in0=o[0:64, :n], in1=pb[0:64, :n], op=ALU.mult),
              r=['fo', 'P0'], w=[ky])
        sc.dma(self.YT[m][row0:row0 + 64, q0:q0 + n], yb[:, :n], r=[ky], w=[('yt', m, row0, q0)])

    def band_batch(self, qt_of, kt_of, va_of, bias_of, blocks, OT, kOT, psb, kpsb, pts, kpts, rkeys):
        nc, sc = self.nc, self.sc
        ident = self.ident
        for j, has_prev in enumerate(blocks):
            cs = slice(j * 128, (j + 1) * 128)
            for pc in (0, 1):
                if pc == 0 and not has_prev:
                    continue
                ps = psb[pc]
                sc.op('pe', lambda ps=ps, j=j, pc=pc, cs=cs: nc.tensor.matmul(
                    ps[:, cs], lhsT=kt_of(j, pc), rhs=qt_of(j), start=True, stop=False),
                    r=rkeys, w=[kpsb[pc]], sig=False)
                sc.op('pe', lambda ps=ps, pc=pc, cs=cs: nc.tensor.matmul(
                    ps[:, cs], lhsT=ident[:, :], rhs=bias_of(pc), start=False, stop=True),
                    r=rkeys + ['ident'], w=[kpsb[pc]])
        for j, has_prev in enumerate(blocks):
            cs = slice(j * 128, (j + 1) * 128)
            for pc in (0, 1):
                if pc == 0 and not has_prev:
                    continue
                sc.op('act', lambda pc=pc, cs=cs: nc.scalar.activation(out=pts[pc][:, cs], in_=psb[pc][:, cs], func=AF.Exp),
                      r=[kpsb[pc]], w=[kpts[pc]])
        for j, has_prev in enumerate(blocks):
            cs = slice(j * 128, (j + 1) * 128)
            first = True
            for pc in (0, 1):
                if pc == 0 and not has_prev:
                    continue
                sc.op('pe', lambda pc=pc, cs=cs, j=j, first=first: nc.tensor.matmul(
                    OT[0:65, cs], lhsT=va_of(j, pc), rhs=pts[pc][:, cs], start=first, stop=(pc == 1)),
                    r=rkeys + [kpts[pc]], w=[kOT], sig=(pc == 1))
                first = False

    def phaseC(self, l):
        nc, sc, S = self.nc, self.sc, self.S
        P = self.P
        NT = self.NT
        with ExitStack() as es:
            sb = self._alloc(es, "pc%d_" % l)
            kt = [sb("kt%d" % i, [96, S], BF16) for i in range(2)]
            va = [sb("va%d" % i, [128, NT, 65], BF16) for i in range(2)]
            qt = [sb("qt%d" % i, [96, 512], BF16) for i in range(2)]
            pt = [sb("pt%d" % i, [128, 512], BF16) for i in range(2)]
            F = self._fin_bufs(sb)
            for i in range(2):
                sc.op('dve', lambda i=i: nc.vector.memset(va[i][:, :, 64:65], 1.0), w=['va%d' % i])
            cnt = 0
            for h in range(8):
                i = h % 2
                self.wload(kt[i], self.KT_C[h], S, 'kt%d' % i)
                vsrc = self.V_C.rearrange("(n p) c -> p n c", p=128)
                self.vload(lambda a, b_, i=i: va[i][:, a:b_, 0:64], lambda a, b_, h=h: vsrc[:, a:b_, h * 64:(h + 1) * 64],
                           NT, 'va%d' % i)
                for qg in range(S // 512):
                    qi = cnt % 2
                    cnt += 1
                    sc.dma(qt[qi][:, :], self.QT_C[h][:, qg * 512:(qg + 1) * 512], w=['qt%d' % qi])
                    OT, kOT = P[4 + qi], 'P%d' % (4 + qi)
                    nkb = 4 * qg + 4
                    for kb in range(nkb):
                        m = max(0, kb - 4 * qg)
                        c0 = 128 * m
                        pi = 2 + kb % 2
                        ps, kps = P[pi], 'P%d' % pi
                        diag = kb >= 4 * qg
                        sc.op('pe', lambda ps=ps, kb=kb, c0=c0, diag=diag: nc.tensor.matmul(
                            ps[:, c0:512], lhsT=kt[i][:, kb * 128:(kb + 1) * 128], rhs=qt[qi][:, c0:512],
                            start=True, stop=not diag), r=['kt%d' % i, 'qt%d' % qi], w=[kps], sig=not diag)
                        if diag:
                            sc.op('pe', lambda ps=ps, c0=c0: nc.tensor.matmul(
                                ps[:, c0:c0 + 128], lhsT=self.ident[:, :], rhs=self.caus_kq[:, :], start=False, stop=True),
                                r=['ident', 'caus_kq'], w=[kps])
                        pti = kb % 2
                        sc.op('act', lambda ps=ps, c0=c0, pti=pti: nc.scalar.activation(
                            out=pt[pti][:, c0:512], in_=ps[:, c0:512], func=AF.Exp), r=[kps], w=['pt%d' % pti])
                        sc.op('pe', lambda kb=kb, c0=c0, pti=pti, OT=OT, nkb=nkb: nc.tensor.matmul(
                            OT[0:65, c0:512], lhsT=va[i][:, kb, :], rhs=pt[pti][:, c0:512],
                            start=(kb == 0), stop=(kb == nkb - 1)), r=['va%d' % i, 'pt%d' % pti], w=[kOT],
                            sig=(kb == nkb - 1))
                    self.finish_head(F, OT[0:65, :], kOT, 2, h * 64, qg * 512)
        sc.barrier()

    def phaseD(self, l):
        nc, sc, S = self.nc, self.sc, self.S
        P = self.P
        NT = self.NT
        HO = (0, 4, 1, 5, 2, 6, 3, 7)
        with ExitStack() as es:
            sb = self._alloc(es, "pd%d_" % l)
            kt = sb("kt", [128, S], BF16)
            va = sb("va", [128, NT, 2, 65], BF16)
            bst = sb("bst", [128, 8 * 2 * 128], F32)
            bias = sb("bias", [128, 8, 2, 128], BF16)
            qt = [sb("qt%d" % i, [128, 4, 512], BF16) for i in range(2)]
            pts = [[sb("pt%d%d" % (a, b), [128, 512], BF16) for b in range(2)] for a in range(2)]
            snk = sb("snk", [65, 8], F32)
            esk = sb("esk", [65, 8], F32)
            F = self._fin_bufs(sb)
            sc.op('dve', lambda: nc.vector.memset(va[:, :, :, 64:65], 1.0), w=['va'])
            self.wload(kt, self.KT_D, S, 'kt')
            vsrc = self.V_D.rearrange("(n p) c -> p n c", p=128)
            for g_ in range(2):
                self.vload(lambda a, b_, g_=g_: va[:, a:b_, g_, 0:64], lambda a, b_, g_=g_: vsrc[:, a:b_, g_ * 64:(g_ + 1) * 64],
                           NT, 'va')
            sc.dma(bst[:, :], self.biasD[:, :], w=['bst'])
            sc.op('pool', lambda: nc.gpsimd.tensor_copy(out=bias[:].rearrange("k h p q -> k (h p q)"), in_=bst[:, :]),
                  r=['bst'], w=['bias'])
            sc.dma(snk[64:65, :], self.pv[l:l + 1, NPV - 8:NPV], w=['snk'])
            sc.op('act', lambda: nc.scalar.activation(out=esk[64:65, :], in_=snk[64:65, :], func=AF.Exp),
                  r=['snk'], w=['esk'])
            for qg in range(S // 512):
                qi = qg % 2
                for p_ in range(4):
                    sc.dma(qt[qi][:, p_, :], self.QT_D[p_ * 128:(p_ + 1) * 128, qg * 512:(qg + 1) * 512], w=['qt%d' % qi])
                for r in range(8):
                    h, p, half = HO[r], r // 2, r % 2
                    b = 64 * half
                    a = r % 2
                    OT, kOT = P[4 + a], 'P%d' % (4 + a)
                    psb = (P[2 * a], P[2 * a + 1])
                    kpsb = ('P%d' % (2 * a), 'P%d' % (2 * a + 1))
                    blocks = [(4 * qg + j) > 0 for j in range(4)]
                    self.band_batch(
                        qt_of=lambda j: qt[qi][b:b + 64, p, j * 128:(j + 1) * 128],
                        kt_of=lambda j, pc: kt[b:b + 64, (4 * qg + j - 1 + pc) * 128:(4 * qg + j + pc) * 128],
                        va_of=lambda j, pc: va[:, 4 * qg + j - 1 + pc, half, :],
                        bias_of=lambda pc: bias[:, h, pc, :],
                        blocks=blocks, OT=OT, kOT=kOT, psb=psb, kpsb=kpsb, pts=pts[a],
                        kpts=('pt%d0' % a, 'pt%d1' % a), rkeys=['kt', 'va', 'bias', 'qt%d' % qi])
                    self.finish_head(F, OT[0:65, :], kOT, 3, h * 64, qg * 512, sink=esk[64:65, h:h + 1])
        sc.barrier()

    def phaseB(self, l):
        nc, sc, S = self.nc, self.sc, self.S
        P = self.P
        with ExitStack() as es:
            sb = self._alloc(es, "pb%d_" % l)
            ktw = sb("ktw", [128, 4, 2 * TG], BF16)
            vaw = sb("vaw", [128, 32, 8, 65], BF16)
            qt = sb("qt", [128, 4, TG], BF16)
            bst = sb("bst", [128, 8 * 2 * 128], F32)
            bias = sb("bias", [128, 8, 2, 128], BF16)
            acc = [sb("acc%d" % h, [65, TG], F32) for h in range(8)]
            pts = [[sb("pt%d%d" % (a, b), [128, 512], BF16) for b in range(2)] for a in range(2)]
            F = self._fin_bufs(sb)
            sc.op('dve', lambda: nc.vector.memset(vaw[:, :, :, 64:65], 1.0), w=['vaw'])
            for tg in range(self.NG):
                t0 = tg * TG
                for g, d in enumerate(B_DIL):
                    ktv = self.KT_B[g]
                    vv = self.V_B[g].rearrange("(n p) c -> p n c", p=128)
                    if tg > 0:
                        for p_ in range(4):
                            for hb in range(2):
                                sc.dma(ktw[:, p_, hb * TG:(hb + 1) * TG],
                                       ktv[p_ * 128:(p_ + 1) * 128, t0 - TG + hb * TG:t0 + hb * TG], w=['ktw'])
                        for h_ in range(8):
                            for hb in range(2):
                                sc.dma(vaw[:, 16 * hb:16 * hb + 16, h_, 0:64],
                                       vv[:, 16 * tg - 16 + 16 * hb:16 * tg + 16 * hb, h_ * 64:(h_ + 1) * 64], w=['vaw'])
                    else:
                        for p_ in range(4):
                            sc.dma(ktw[:, p_, TG:2 * TG], ktv[p_ * 128:(p_ + 1) * 128, 0:TG], w=['ktw'])
                        for h_ in range(8):
                            sc.dma(vaw[:, 16:32, h_, 0:64], vv[:, 0:16, h_ * 64:(h_ + 1) * 64], w=['vaw'])
                    for p_ in range(4):
                        sc.dma(qt[:, p_, :], self.QT_B[g][p_ * 128:(p_ + 1) * 128, t0:t0 + TG], w=['qt'])
                    if tg == 0:
                        sc.dma(bst[:, :], self.biasB[:, g * 2048:(g + 1) * 2048], w=['bst'])
                        sc.op('pool', lambda: nc.gpsimd.tensor_copy(out=bias[:].rearrange("k h p q -> k (h p q)"),
                                                                    in_=bst[:, :]), r=['bst'], w=['bias'])
                    elif g == 0 or True:
                        sc.dma(bst[:, :], self.biasB[:, g * 2048:(g + 1) * 2048], w=['bst'])
                        sc.op('pool', lambda: nc.gpsimd.tensor_copy(out=bias[:].rearrange("k h p q -> k (h p q)"),
                                                                    in_=bst[:, :]), r=['bst'], w=['bias'])

                    def prev_of(nl):
                        if d == 1:
                            return (16 * tg + nl) > 0, 16 + nl - 1
                        if d == 4:
                            if nl % 4 > 0:
                                return True, 16 + nl - 1
                            return tg > 0, 16 + nl - 13
                        return tg > 0, nl
                    for bb in range(4):
                        for h in range(8):
                            p, b = h // 2, 64 * (h % 2)
                            a = h % 2
                            OT, kOT = P[4 + a], 'P%d' % (4 + a)
                            psb = (P[2 * a], P[2 * a + 1])
                            kpsb = ('P%d' % (2 * a), 'P%d' % (2 * a + 1))
                            pv_ = [prev_of(4 * bb + j) for j in range(4)]
                            blocks = [x[0] for x in pv_]

                            def widx(j, pc):
                                return (16 + 4 * bb + j) if pc == 1 else pv_[j][1]
                            self.band_batch(
                                qt_of=lambda j: qt[b:b + 64, p, (4 * bb + j) * 128:(4 * bb + j + 1) * 128],
                                kt_of=lambda j, pc: ktw[b:b + 64, p, widx(j, pc) * 128:(widx(j, pc) + 1) * 128],
                                va_of=lambda j, pc: vaw[:, widx(j, pc), h, :],
                                bias_of=lambda pc: bias[:, h, pc, :],
                                blocks=blocks, OT=OT, kOT=kOT, psb=psb, kpsb=kpsb, pts=pts[a],
                                kpts=('pt%d0' % a, 'pt%d1' % a), rkeys=['ktw', 'vaw', 'bias', 'qt'])
                            ka = 'acc%d' % h
                            if d == 1:
                                sc.op('act', lambda h=h, bb=bb, OT=OT: nc.scalar.copy(
                                    out=acc[h][0:65, bb * 512:(bb + 1) * 512], in_=OT[0:65, :]), r=[kOT], w=[ka])
                            elif d == 4:
                                dst = acc[h][0:65, :].rearrange("c (i r) -> c r i", r=4)[:, bb, :]
                                sc.op('dve', lambda dst=dst, OT=OT: nc.vector.tensor_tensor(
                                    out=dst, in0=dst, in1=OT[0:65, :], op=ALU.add), r=[kOT, ka], w=[ka])
                            else:
                                dst = acc[h][0:65, :].rearrange("c (i r) -> c r i", r=16)[:, 4 * bb:4 * bb + 4, :]
                                sc.op('dve', lambda dst=dst, OT=OT: nc.vector.tensor_tensor(
                                    out=dst, in0=dst, in1=OT[0:65, :].rearrange("c (j i) -> c j i", j=4), op=ALU.add),
                                    r=[kOT, ka], w=[ka])
                for h in range(8):
                    for ch in range(4):
                        self.finish_head(F, acc[h][0:65, ch * 512:(ch + 1) * 512], 'acc%d' % h, 1, h * 64,
                                         t0 + ch * 512)
        sc.barrier()

    def phaseA(self, l):
        nc, sc, S = self.nc, self.sc, self.S
        P = self.P
        NT = self.NT
        ident = self.ident
        with ExitStack() as es:
            sb = self._alloc(es, "pa%d_" % l)
            ikT = sb("ikT", [128, S], BF16)
            IS = sb("IS", [128, S], F32)
            NM = sb("NM", [128, 4, S], BF16)
            iqT = sb("iqT", [128, 2, 512], BF16)
            iqx = sb("iqx", [32, 2, 512], BF16)
            iw = sb("iw", [128, 4, 8], F32)
            R = [sb("R%d" % i, [128, 512], F32) for i in range(2)]
            mabs, mid, cnt, tt_ = (sb(n_, [128, 1], F32) for n_ in ("mabs", "mid", "cnt", "tt"))
            H = sb("H", [128, 32], F32)
            Hn = sb("Hn", [128, 32], F32)
            thr = sb("thr", [128, 1], F32)
            kt = sb("kt", [128, S], BF16)
            va = sb("va", [128, NT, 2, 65], BF16)
            bst = sb("bst", [128, 13 * 128], F32)
            bias = sb("bias", [128, 8, 13, 128], BF16)
            b31 = sb("b31", [128, 8], F32)
            qt = sb("qt", [128, 512], BF16)
            pt = [sb("pt%d" % i, [128, 512], BF16) for i in range(2)]
            F = self._fin_bufs(sb)
            sc.op('dve', lambda: nc.vector.memset(va[:, :, :, 64:65], 1.0), w=['va'])
            self.wload(ikT, self.IKT, S, 'ikT')
            sc.dma(b31[:, :], self.b31[:, :], w=['b31'])
            for h in range(8):
                sc.dma(bst[:, :], self.biasA[h], w=['bst'])
                sc.op('pool', lambda h=h: nc.gpsimd.tensor_copy(out=bias[:, h].rearrange("k d q -> k (d q)"), in_=bst[:, :]),
                      r=['bst'], w=['bias'])
            for qg in range(S // 512):
                q0 = qg * 512
                for p_ in range(2):
                    sc.dma(iqT[:, p_, :], self.IQT[p_ * 128:(p_ + 1) * 128, q0:q0 + 512], w=['iqT'])
                    sc.dma(iqx[:, p_, :], self.IQT[p_ * 128 + 96:(p_ + 1) * 128, q0:q0 + 512], w=['iqT'])
                sc.dma(iw[:, :, :], self.IW[q0:q0 + 512, :].rearrange("(j p) e -> p j e", p=128), w=['iw'])
                for j in range(4):
                    qb = 4 * qg + j
                    N = (qb + 1) * 128
                    for kc in range((N + 511) // 512):
                        w_ = min(512, N - 512 * kc)
                        for h in range(8):
                            pi = h % 2
                            bp = 32 * (h % 4)
                            if bp == 96:
                                lhs_ = iqx[0:32, h // 4, j * 128:(j + 1) * 128]
                                bp = 0
                            else:
                                lhs_ = iqT[bp:bp + 32, h // 4, j * 128:(j + 1) * 128]
                            sc.op('pe', lambda pi=pi, bp=bp, h=h, kc=kc, w_=w_, j=j, lhs_=lhs_: nc.tensor.matmul(
                                P[pi][:, :w_], lhsT=lhs_,
                                rhs=ikT[bp:bp + 32, kc * 512:kc * 512 + w_], start=True, stop=True),
                                r=['iqT', 'ikT'], w=['P%d' % pi])
                            sc.op('act', lambda pi=pi, w_=w_: nc.scalar.activation(
                                out=R[pi][:, :w_], in_=P[pi][:, :w_], func=AF.Relu), r=['P%d' % pi], w=['R%d' % pi])
                            dst = IS[:, kc * 512:kc * 512 + w_]
                            if h == 0:
                                sc.op('dve', lambda pi=pi, w_=w_, dst=dst, j=j: nc.vector.tensor_scalar(
                                    out=dst, in0=R[pi][:, :w_], scalar1=iw[:, j, 0:1], scalar2=None, op0=ALU.mult),
                                    r=['R%d' % pi, 'iw'], w=['IS'])
                            else:
                                sc.op('dve', lambda pi=pi, w_=w_, dst=dst, j=j, h=h: nc.vector.scalar_tensor_tensor(
                                    out=dst, in0=R[pi][:, :w_], scalar=iw[:, j, h:h + 1], in1=dst, op0=ALU.mult,
                                    op1=ALU.add), r=['R%d' % pi, 'iw', 'IS'], w=['IS'])
                    sc.op('dve', lambda N=N: nc.vector.tensor_reduce(out=mabs[:, :], in_=IS[:, :N], axis=AX.X, op=ALU.max,
                                                                     apply_absolute_value=True), r=['IS'], w=['mabs'])
                    sc.op('dve', lambda qb=qb, N=N: nc.vector.tensor_tensor(
                        out=IS[:, qb * 128:N], in0=IS[:, qb * 128:N], in1=self.causneg, op=ALU.add),
                        r=['IS', 'cst_f'], w=['IS'])
                    sc.op('dve', lambda: nc.vector.tensor_scalar(out=H[:, :], in0=self.pow2, scalar1=mabs[:, 0:1],
                                                                 scalar2=1.0009765625, op0=ALU.mult, op1=ALU.mult),
                          r=['mabs', 'cst_f'], w=['H'])
                    sc.op('dve', lambda: nc.vector.tensor_scalar(out=Hn[:, :], in0=H[:, :], scalar1=-1.0, scalar2=None,
                                                                 op0=ALU.mult), r=['H'], w=['Hn'])
                    sc.op('dve', lambda: nc.vector.memset(mid[:, :], 0.0), w=['mid'])
                    for n in range(NBIS):
                        sc.op('dve', lambda N=N, j=j: nc.vector.tensor_scalar(
                            out=NM[:, j, :N], in0=IS[:, :N], scalar1=mid[:, 0:1], scalar2=None, op0=ALU.is_ge,
                            op1=ALU.add, accum_out=cnt[:, 0:1]), r=['IS', 'mid'], w=['NM%d' % j, 'cnt'])
                        sc.op('dve', lambda n=n: nc.vector.tensor_scalar(
                            out=tt_[:, :], in0=cnt[:, :], scalar1=TOPK - 0.5, scalar2=H[:, n:n + 1], op0=ALU.is_ge,
                            op1=ALU.mult), r=['cnt', 'H'], w=['tt'])
                        sc.op('dve', lambda n=n: nc.vector.scalar_tensor_tensor(
                            out=mid[:, :], in0=tt_[:, :], scalar=Hn[:, n + 1:n + 2], in1=mid[:, :], op0=ALU.add,
                            op1=ALU.add), r=['tt', 'Hn', 'mid'], w=['mid'])
                    sc.op('dve', lambda: nc.vector.tensor_tensor(out=thr[:, :], in0=mid[:, :], in1=Hn[:, NBIS:NBIS + 1],
                                                                 op=ALU.add), r=['mid', 'Hn'], w=['thr'])
                    if self.dbg:
                        sc.dma(self.THR[qb * 128:(qb + 1) * 128, :], thr[:, :], r=['thr'], w=[('thrd', qb)])
                    sc.op('dve', lambda N=N, j=j: nc.vector.tensor_scalar(
                        out=NM[:, j, :N], in0=IS[:, :N], scalar1=thr[:, 0:1], scalar2=NEG, op0=ALU.is_lt, op1=ALU.mult),
                        r=['IS', 'thr'], w=['NM%d' % j])
                nkb = 4 * qg + 4
                nmk = ['NM%d' % j for j in range(4)]
                for hp in range(4):
                    self.wload(kt, self.KT_A[hp * 128:(hp + 1) * 128, :], nkb * 128, 'kt')
                    vsrc = self.V_A.rearrange("(n p) c -> p n c", p=128)
                    for e_ in range(2):
                        hh_ = 2 * hp + e_
                        self.vload(lambda a, b_, e_=e_: va[:, a:b_, e_, 0:64],
                                   lambda a, b_, hh_=hh_: vsrc[:, a:b_, hh_ * 64:(hh_ + 1) * 64], nkb, 'va')
                    sc.dma(qt[:, :], self.QT_A[hp * 128:(hp + 1) * 128, q0:q0 + 512], w=['qt'])
                    for e in range(2):
                        h = 2 * hp + e
                        b = 64 * e
                        OT, kOT = P[4 + e], 'P%d' % (4 + e)
                        for kb in range(nkb):
                            m = max(0, kb - 4 * qg)
                            c0 = 128 * m
                            pi = 2 + kb % 2
                            ps, kps = P[pi], 'P%d' % pi
                            mm = []
                            mm.append((lambda s_, t_, ps=ps, kb=kb, c0=c0: nc.tensor.matmul(
                                ps[:, c0:512], lhsT=kt[b:b + 64, kb * 128:(kb + 1) * 128], rhs=qt[b:b + 64, c0:512],
                                start=s_, stop=t_), ['kt', 'qt']))
                            jsplit = 4
                            for jj in range(m, 4):
                                cs = slice(jj * 128, (jj + 1) * 128)
                                mm.append((lambda s_, t_, ps=ps, jj=jj, kb=kb, cs=cs: nc.tensor.matmul(
                                    ps[:, cs], lhsT=NM[:, jj, kb * 128:(kb + 1) * 128], rhs=ident[:, :], start=s_, stop=t_),
                                    ['NM%d' % jj, 'ident']))
                                D = 4 * qg + jj - kb
                                if D < 13:
                                    mm.append((lambda s_, t_, ps=ps, cs=cs, D=D, h=h: nc.tensor.matmul(
                                        ps[:, cs], lhsT=ident[:, :], rhs=bias[:, h, D, :], start=s_, stop=t_),
                                        ['bias', 'ident']))
                                elif jsplit == 4:
                                    jsplit = jj
                            for ii, (fn, rk) in enumerate(mm):
                                last = ii == len(mm) - 1
                                sc.op('pe', lambda fn=fn, ii=ii, last=last: fn(ii == 0, last), r=rk, w=[kps], sig=last)
                            pti = kb % 2
                            cn = 128 * jsplit
                            if cn > c0:
                                sc.op('act', lambda ps=ps, c0=c0, cn=cn, pti=pti: nc.scalar.activation(
                                    out=pt[pti][:, c0:cn], in_=ps[:, c0:cn], func=AF.Exp), r=[kps], w=['pt%d' % pti])
                            if cn < 512:
                                cf = max(cn, c0)
                                sc.op('act', lambda ps=ps, cf=cf, pti=pti, h=h: nc.scalar.activation(
                                    out=pt[pti][:, cf:512], in_=ps[:, cf:512], func=AF.Exp, bias=b31[:, h:h + 1]),
                                    r=[kps, 'b31'], w=['pt%d' % pti])
                            sc.op('pe', lambda kb=kb, c0=c0, pti=pti, OT=OT, e=e: nc.tensor.matmul(
                                OT[0:65, c0:512], lhsT=va[:, kb, e, :], rhs=pt[pti][:, c0:512],
                                start=(kb == 0), stop=(kb == nkb - 1)), r=['va', 'pt%d' % pti], w=[kOT],
                                sig=(kb == nkb - 1))
                        self.finish_head(F, OT[0:65, :], kOT, 0, h * 64, q0)
        sc.barrier()

    def phase3(self, l, Xsrc, Xdst, keep=(0, 1, 2, 3)):
        nc, sc, S = self.nc, self.sc, self.S
        P = self.P
        with ExitStack() as es:
            sb = self._alloc(es, "p3%d_" % l)
            wst = sb("wst", [128, 4, 1024], F32)
            wbr_b = sb("wbr_b", [128, 4, 4, 1024], BF16)
            wout_b = sb("wout_b", [128, 8, 1024], BF16)
            yt = sb("yt", [128, 4, 4, 512], BF16)
            zt = sb("zt", [128, 4, 4, 512], BF16)
            gt = sb("gt", [128, 32, 512], BF16)
            mT = sb("mT", [128, 8, 512], BF16)
            tmp = [sb("tmp%d" % i, [128, 512], F32) for i in range(2)]
            macc = sb("macc", [128, 512], F32)
            xt = [sb("xt%d" % i, [128, 1024], F32) for i in range(2)]
            xo = [sb("xo%d" % i, [128, 1024], F32) for i in range(2)]
            for b in range(4):
                sc.dma(wst[:, :, :], self.wbr[l, b].rearrange("(c p) n -> p c n", p=128), w=['wst'])
                sc.op('pool', lambda b=b: nc.gpsimd.tensor_copy(out=wbr_b[:, b], in_=wst[:, :, :]), r=['wst'], w=['wbr_b'])
            for hf in range(2):
                sc.dma(wst[:, :, :], self.wout[l, hf * 512:(hf + 1) * 512, :].rearrange("(c p) n -> p c n", p=128),
                       w=['wst'])
                sc.op('pool', lambda hf=hf: nc.gpsimd.tensor_copy(out=wout_b[:, hf * 4:(hf + 1) * 4, :], in_=wst[:, :, :]),
                      r=['wst'], w=['wout_b'])
            cnt = 0
            for tq in range(S // 512):
                c0 = tq * 512
                for m in keep:
                    for c_ in range(4):
                        sc.dma(yt[:, m, c_, :], self.YT[m][c_ * 128:(c_ + 1) * 128, c0:c0 + 512], w=['yt'])
                        sc.dma(zt[:, m, c_, :], self.ZGT[m * 512 + c_ * 128:m * 512 + (c_ + 1) * 128, c0:c0 + 512], w=['zt'])
                for k_ in range(32):
                    if (k_ // 8) in keep:
                        sc.dma(gt[:, k_, :], self.ZGT[2048 + k_ * 128:2048 + (k_ + 1) * 128, c0:c0 + 512], w=['gt'])
                for m in keep:
                    sc.op('dve', lambda m=m: nc.vector.tensor_tensor(out=yt[:, m], in0=yt[:, m], in1=zt[:, m], op=ALU.mult),
                          r=['yt', 'zt'], w=['yt'])
                for oc in range(8):
                    for b in keep:
                        pi = b % 2
                        for c in range(4):
                            sc.op('pe', lambda pi=pi, b=b, c=c, oc=oc: nc.tensor.matmul(
                                P[pi][:, :], lhsT=wbr_b[:, b, c, oc * 128:(oc + 1) * 128], rhs=yt[:, b, c, :],
                                start=(c == 0), stop=(c == 3)), r=['wbr_b', 'yt'], w=['P%d' % pi], sig=(c == 3))
                        if b == keep[0]:
                            sc.op('dve', lambda pi=pi, oc=oc, b=b: nc.vector.tensor_tensor(
                                out=macc[:, :], in0=P[pi][:, :], in1=gt[:, b * 8 + oc, :], op=ALU.mult),
                                r=['P%d' % pi, 'gt'], w=['macc'])
                        else:
                            ti = b % 2
                            sc.op('dve', lambda pi=pi, oc=oc, b=b, ti=ti: nc.vector.tensor_tensor(
                                out=tmp[ti][:, :], in0=P[pi][:, :], in1=gt[:, b * 8 + oc, :], op=ALU.mult),
                                r=['P%d' % pi, 'gt'], w=['tmp%d' % ti])
                            sc.op('dve', lambda ti=ti: nc.vector.tensor_tensor(
                                out=macc[:, :], in0=macc[:, :], in1=tmp[ti][:, :], op=ALU.add),
                                r=['macc', 'tmp%d' % ti], w=['macc'])
                    sc.op('act', lambda oc=oc: nc.scalar.copy(out=mT[:, oc, :], in_=macc[:, :]), r=['macc'], w=['mT'])
                for ti in range(4):
                    i = cnt % 2
                    cnt += 1
                    r0 = c0 + ti * 128
                    sc.dma(xt[i][:, :], Xsrc[r0:r0 + 128, :], w=['xt%d' % i])
                    for hf in range(2):
                        pi = 2 + hf
                        for oc in range(8):
                            sc.op('pe', lambda pi=pi, oc=oc, ti=ti, hf=hf: nc.tensor.matmul(
                                P[pi][:, :], lhsT=mT[:, oc, ti * 128:(ti + 1) * 128], rhs=wout_b[:, oc, hf * 512:(hf + 1) * 512],
                                start=(oc == 0), stop=(oc == 7)), r=['mT', 'wout_b'], w=['P%d' % pi], sig=(oc == 7))
                        sc.op('dve', lambda pi=pi, hf=hf, i=i: nc.vector.tensor_tensor(
                            out=xo[i][:, hf * 512:(hf + 1) * 512], in0=P[pi][:, :], in1=xt[i][:, hf * 512:(hf + 1) * 512],
                            op=ALU.add), r=['P%d' % pi, 'xt%d' % i], w=['xo%d' % i])
                    sc.dma(Xdst[r0:r0 + 128, :], xo[i][:, :], r=['xo%d' % i], w=[('xd', r0)])
        sc.barrier()

    def layer(self, l, Xsrc, Xdst, mixers="ABCD"):
        self.phase1(l, Xsrc)
        self.sc.barrier()
        if "A" in mixers:
            self.phaseA(l)
        if "B" in mixers:
            self.phaseB(l)
        if "C" in mixers:
            self.phaseC(l)
        if "D" in mixers:
            self.phaseD(l)
        self.phase3(l, Xsrc, Xdst, tuple('ABCD'.index(c) for c in mixers))


    def finish(self):
        sc = self.sc
        sc.barrier()


def host_inputs(inputs, S, L):
    f = np.float32
    cols = _w_in_cols()
    w_in = np.asarray(inputs['w_in'], f)
    wpad = np.concatenate([w_in, np.zeros((L, 1024, 1), f)], axis=2)
    w_in_p = np.ascontiguousarray(wpad[:, :, np.where(cols >= 0, cols, w_in.shape[2])])
    wkvb = np.asarray(inputs['w_kv_b'], f).reshape(L, 128, 8, 2, 64).transpose(0, 1, 3, 2, 4).reshape(L, 128, 1024)
    pv = np.concatenate([
        np.asarray(inputs['norm_gain'], f).reshape(L, -1),
        np.asarray(inputs['qk_gain_a'], f).reshape(L, -1),
        np.asarray(inputs['qk_gain_b'], f).reshape(L, -1),
        np.asarray(inputs['qk_gain_c'], f).reshape(L, -1),
        np.asarray(inputs['qk_gain_d'], f).reshape(L, -1),
        np.asarray(inputs['c_q_gain'], f).reshape(L, -1),
        np.asarray(inputs['c_kv_gain'], f).reshape(L, -1),
        np.asarray(inputs['sinks'], f).reshape(L, -1)], axis=1)
    assert pv.shape[1] == NPV
    ii = np.arange(128)
    ident = np.eye(128, dtype=f)
    causneg = np.where(ii[None, :] <= ii[:, None], 0.0, -BIG).astype(f)
    caus_kq = np.where(ii[:, None] <= ii[None, :], 0.0, NEG).astype(f)
    pow2 = np.broadcast_to((2.0 ** -np.arange(32)).astype(f)[None, :], (128, 32))
    cst = np.ascontiguousarray(np.concatenate([ident, causneg, caus_kq, pow2], axis=1))
    half = 16
    freq = (np.float32(10000.0) ** (-np.arange(half, dtype=f) / half)).astype(f)
    ang = np.arange(S, dtype=f)[:, None] * freq[None, :]
    rope = np.concatenate([np.cos(ang), np.sin(ang)], axis=1).astype(f)
    rb = np.asarray(inputs['rel_bias'], f)
    kk, qq = ii[:, None], ii[None, :]
    bA = np.zeros((8, 128, 13, 128), f)
    for D in range(13):
        dist = 128 * D + qq - kk
        g = rb[_t5_bucket(dist)][:, :, 0:8]
        if D == 0:
            g = np.where((dist >= 0)[:, :, None], g, f(NEG))
        bA[:, :, D, :] = g.transpose(2, 0, 1)
    b31 = np.ascontiguousarray(np.broadcast_to(rb[31, 0:8][None, :], (128, 8)))
    def band(table, step, max_dist):
        out = np.zeros((128, table.shape[1], 2, 128), f)
        for pc in range(2):
            rel = (128 if pc == 0 else 0) + qq - kk
            ok = (rel >= 0) & (rel <= max_dist)
            g = table[_t5_bucket(rel * step)]
            g = np.where(ok[:, :, None], g, f(NEG))
            out[:, :, pc, :] = g.transpose(0, 2, 1)
        return out
    bB = np.stack([band(rb[:, 8 + g * 8: 16 + g * 8], B_DIL[g], 128) for g in range(3)], axis=1)
    bD = band(rb[:, 32:40], 1, 127)
    return dict(w_in=w_in_p, wqb=np.ascontiguousarray(np.asarray(inputs['w_q_b'], f)),
                wkvb=np.ascontiguousarray(wkvb), wbr=np.ascontiguousarray(np.asarray(inputs['w_branch'], f)),
                wout=np.ascontiguousarray(np.asarray(inputs['w_out'], f)), pv=np.ascontiguousarray(pv), cst=cst,
                rope=rope, biasA=np.ascontiguousarray(bA.reshape(8, 128, 13 * 128)), b31=b31,
                biasB=np.ascontiguousarray(bB.reshape(128, -1)), biasD=np.ascontiguousarray(bD.reshape(128, -1)))


_CACHE = {}


def _get_prog(S, L):
    key = (S, L)
    if key not in _CACHE:
        p = Prog(S, L)
        src = p.x_in
        for l in range(L):
            dst = p.y_out if l == L - 1 else p.X[l % 2]
            p.layer(l, src, dst)
            src = dst
        p.finish()
        _CACHE[key] = p
    return _CACHE[key]


def kernel(**inputs):
    x = np.asarray(inputs['x'], np.float32)
    B, S, _ = x.shape
    L = np.asarray(inputs['norm_gain']).shape[0]
    p = _get_prog(S, L)
    shared = host_inputs(inputs, S, L)
    n_cores = 8
    in_maps = []
    for c in range(n_cores):
        m = dict(shared)
        m['x'] = np.ascontiguousarray(x[c % B])
        in_maps.append(m)
    res = run_bass_kernel_spmd(p.nc, in_maps, core_ids=list(range(n_cores)))
    out = np.stack([np.asarray(res.results[b]['y'], np.float32) for b in range(B)], axis=0)
    return out
```

```python
from contextlib import ExitStack
import numpy as np
import ml_dtypes
import concourse.bass as bass
import concourse.mybir as mybir
from concourse.bass_utils import run_bass_kernel_spmd

F32 = mybir.dt.float32
BF16 = mybir.dt.bfloat16
AF = mybir.ActivationFunctionType
ALU = mybir.AluOpType
AX = mybir.AxisListType

D_MODEL = 1024
EPS = 1e-6
NEG = -30000.0
BIG = 1.0e30
TG = 2048
NBIS = 26
TOPK = 256
B_DIL = (1, 4, 16)
NCOLG = 28
NPV = 1024 + 64 * 6 + 96 * 2 + 256 + 128 + 8

A0, B0_, C0, D0, G0 = 0, 2344, 7464, 8392, 9672


def _w_in_cols():
    groups = []
    ar = np.arange
    groups.append(A0 + 1536 + ar(512))
    groups.append(B0_ + 4608 + ar(512))
    groups.append(C0 + 416 + ar(512))
    groups.append(D0 + 768 + ar(512))
    for n in range(8):
        groups.append(G0 + n * 512 + ar(512))
    groups.append(A0 + ar(512))
    groups.append(A0 + 512 + ar(512))
    groups.append(A0 + 1024 + ar(512))
    for g in range(3):
        groups.append(B0_ + g * 512 + ar(512))
    for g in range(3):
        groups.append(B0_ + 1536 + g * 512 + ar(512))
    for g in range(3):
        groups.append(B0_ + 3072 + g * 512 + ar(512))
    dq = np.concatenate([D0 + h * 64 + ar(64) for h in (0, 4, 1, 5, 2, 6, 3, 7)])
    groups.append(dq)
    groups.append(np.concatenate([D0 + 512 + ar(128), D0 + 640 + ar(128), A0 + 2048 + ar(256)]))
    pad = -np.ones(88, dtype=np.int64)
    groups.append(np.concatenate([C0 + ar(256), C0 + 256 + ar(128), C0 + 384 + ar(32),
                                  A0 + 2336 + ar(8), pad]))
    ik = A0 + 2304 + ar(32)
    groups.append(np.concatenate([ik, ik, ik, ik, -np.ones(384, dtype=np.int64)]))
    cols = np.concatenate(groups)
    assert cols.shape[0] == NCOLG * 512
    return cols


def _t5_bucket(dist):
    d = np.maximum(dist, 0)
    logd = np.log(np.maximum(d, 1).astype(np.float32) / np.float32(16))
    large = 16 + (logd / np.float32(np.log(2048 / 16)) * np.float32(16)).astype(np.int32)
    return np.where(d < 16, d, np.minimum(large, 31))


class Sched:
    def __init__(self, nc, n_dma=24):
        self.nc = nc
        self.e = dict(pe=nc.tensor, act=nc.scalar, dve=nc.vector, pool=nc.gpsimd, sp=nc.sync)
        self.sem = {k: nc.alloc_semaphore("s_" + k) for k in self.e}
        self.tick = {k: 0 for k in self.e}
        self.seen = {k: {} for k in self.e}
        self.dsem = [nc.alloc_semaphore("dm%d" % i) for i in range(n_dma)]
        self.dcnt = [0] * n_dma
        self.dnext = 0
        self.W = {}
        self.R = {}
        self.nins = 0

    def _wait(self, eng, evs):
        for sid, (sem, val) in evs.items():
            if self.seen[eng].get(sid, 0) < val:
                self.e[eng].wait_ge(sem, val)
                self.seen[eng][sid] = val
                self.nins += 1

    def _deps(self, eng, r, w):
        evs = {}

        def add(d):
            for sid, (sem, val) in d.items():
                if sid not in evs or evs[sid][1] < val:
                    evs[sid] = (sem, val)
        for k in r:
            add(self.W.get(k, {}))
            if isinstance(k, str) and k[0] == 'P' and (k[1:].isdigit() or k[1] == 'B'):
                add({sid: v for sid, v in self.R.get(k, {}).items() if sid != eng})
        for k in w:
            add(self.W.get(k, {}))
            add(self.R.get(k, {}))
        if eng == 'pe':
            evs.pop('pe', None)
        return evs

    def _commit(self, ev, r, w):
        sid, sem, val = ev
        for k in w:
            self.W[k] = {sid: (sem, val)}
            self.R[k] = {}
        for k in r:
            d = self.R.setdefault(k, {})
            if sid not in d or d[sid][1] < val:
                d[sid] = (sem, val)

    def op(self, eng, fn, r=(), w=(), sig=True):
        self._wait(eng, self._deps(eng, r, w))
        ins = fn()
        self.nins += 1
        if sig:
            self.tick[eng] += 1
            ins.then_inc(self.sem[eng], 1)
            val = self.tick[eng]
        else:
            val = self.tick[eng] + 1
        self._commit((eng, self.sem[eng], val), r, w)

    def dma(self, out, in_, r=(), w=(), q='sp'):
        i = self.dnext
        self.dnext = (i + 1) % len(self.dsem)
        evs = self._deps(q, r, w)
        sid = 'd%d' % i
        if self.dcnt[i] > 0:
            v = 16 * self.dcnt[i]
            if sid not in evs or evs[sid][1] < v:
                evs[sid] = (self.dsem[i], v)
        self._wait(q, evs)
        self.e[q].dma_start(out=out, in_=in_).then_inc(self.dsem[i], 16)
        self.nins += 1
        self.dcnt[i] += 1
        self._commit((sid, self.dsem[i], 16 * self.dcnt[i]), r, w)

    def barrier(self):
        evs = {}
        for k in self.e:
            if self.tick[k] > 0:
                evs[k] = (self.sem[k], self.tick[k])
        for i, s in enumerate(self.dsem):
            if self.dcnt[i] > 0:
                evs['d%d' % i] = (s, 16 * self.dcnt[i])
        for k in self.e:
            self._wait(k, dict(evs))
        self.W = {}
        self.R = {}


class Prog:
    def __init__(self, S, L, dbg=False):
        self.S, self.L, self.dbg = S, L, dbg
        self.NT = S // 128
        self.NG = S // TG
        assert S % TG == 0
        nc = self.nc = bass.Bass("TRN2", target_bir_lowering=False)
        self.sc = Sched(nc)

        def din(name, shape, dt=F32):
            return nc.dram_tensor(name, list(shape), dt, kind="ExternalInput").ap()

        def dscr(name, shape, dt=BF16):
            kind = "ExternalOutput" if dbg else "Internal"
            return nc.dram_tensor(name, list(shape), dt, kind=kind).ap()
        self.x_in = din("x", [S, 1024])
        self.w_in = din("w_in", [L, 1024, NCOLG * 512])
        self.wqb = din("wqb", [L, 256, 768])
        self.wkvb = din("wkvb", [L, 128, 1024])
        self.wbr = din("wbr", [L, 4, 512, 1024])
        self.wout = din("wout", [L, 1024, 1024])
        self.pv = din("pv", [L, NPV])
        self.cst = din("cst", [128, 128 * 3 + 32])
        self.rope = din("rope", [S, 32])
        self.biasA = din("biasA", [8, 128, 13 * 128])
        self.b31 = din("b31", [128, 8])
        self.biasB = din("biasB", [128, 3 * 8 * 2 * 128])
        self.biasD = din("biasD", [128, 8 * 2 * 128])
        self.y_out = nc.dram_tensor("y", [S, 1024], F32, kind="ExternalOutput").ap()
        self.X = [dscr("X0", [S, 1024], F32), dscr("X1", [S, 1024], F32)]
        self.ZGT = dscr("ZGT", [6144, S])
        self.QT_A = dscr("QT_A", [512, S])
        self.KT_A = dscr("KT_A", [512, S])
        self.V_A = dscr("V_A", [S, 512])
        self.QT_B = [dscr("QT_B%d" % g, [512, S]) for g in range(3)]
        self.KT_B = [dscr("KT_B%d" % g, [512, S]) for g in range(3)]
        self.V_B = [dscr("V_B%d" % g, [S, 512]) for g in range(3)]
        self.QT_C = dscr("QT_C", [8, 96, S])
        self.KT_C = dscr("KT_C", [8, 96, S])
        self.V_C = dscr("V_C", [S, 512])
        self.QT_D = dscr("QT_D", [512, S])
        self.KT_D = dscr("KT_D", [128, S])
        self.V_D = dscr("V_D", [S, 128])
        self.IQT = dscr("IQT", [256, S])
        self.IKT = dscr("IKT", [128, S])
        self.IW = dscr("IW", [S, 8], F32)
        self.YT = [dscr("YT%d" % m, [512, S]) for m in range(4)]
        self.THR = dscr("THR", [S, 1], F32) if dbg else None
        self.P = [nc.alloc_psum_tensor("P%d" % i, [128, 512], F32) for i in range(6)]
        self.PB = [nc.alloc_psum_tensor("PB%d" % i, [128, 1024], BF16) for i in range(2)]
        self.cst_f = nc.alloc_sbuf_tensor("cst_f", [128, 128 * 3 + 32], F32)
        self.ident = nc.alloc_sbuf_tensor("ident", [128, 128], BF16)
        self.caus_kq = nc.alloc_sbuf_tensor("caus_kq", [128, 128], BF16)
        self.ones_f = nc.alloc_sbuf_tensor("ones_f", [128, 64], F32)
        sc = self.sc
        sc.dma(self.cst_f[:], self.cst[:, :], w=['cst_f'])
        sc.op('dve', lambda: nc.vector.tensor_copy(out=self.ident[:], in_=self.cst_f[:, 0:128]),
              r=['cst_f'], w=['ident'])
        sc.op('dve', lambda: nc.vector.tensor_copy(out=self.caus_kq[:], in_=self.cst_f[:, 256:384]),
              r=['cst_f'], w=['caus_kq'])
        sc.op('dve', lambda: nc.vector.memset(self.ones_f[:], 1.0), w=['ones_f'])
        self.causneg = self.cst_f[:, 128:256]
        self.pow2 = self.cst_f[:, 384:416]

    def headnorm(self, T, src, srckey, nh, hd, gain, out, outkey):
        nc, sc = self.nc, self.sc
        n = nh * hd
        sq, ss, ss2, rs, tn = T['sq'], T['ss'], T['ss2'], T['rs'], T['tn']
        sc.op('act', lambda: nc.scalar.activation(out=sq[:, :n], in_=src, func=AF.Square),
              r=[srckey], w=['sq'])
        sc.op('dve', lambda: nc.vector.tensor_reduce(
            out=ss[:, :nh], in_=sq[:, :n].rearrange("p (h d) -> p h d", d=hd), axis=AX.X, op=ALU.add),
            r=['sq'], w=['ss'])
        sc.op('act', lambda: nc.scalar.activation(out=ss2[:, :nh], in_=ss[:, :nh], func=AF.Sqrt,
                                                   scale=1.0 / hd, bias=T['eps'][:, 0:1]),
              r=['ss'], w=['ss2'])
        sc.op('dve', lambda: nc.vector.reciprocal(out=rs[:, :nh], in_=ss2[:, :nh]), r=['ss2'], w=['rs'])
        sc.op('dve', lambda: nc.vector.tensor_tensor(
            out=tn[:, :n].rearrange("p (h d) -> p h d", d=hd),
            in0=src.rearrange("p (h d) -> p h d", d=hd),
            in1=rs[:, :nh].unsqueeze(2).broadcast_to([128, nh, hd]), op=ALU.mult),
            r=[srckey, 'rs'], w=['tn'])
        sc.op('dve', lambda: nc.vector.tensor_tensor(
            out=out.rearrange("p (h d) -> p h d", d=hd),
            in0=tn[:, :n].rearrange("p (h d) -> p h d", d=hd),
            in1=gain.unsqueeze(1).broadcast_to([128, nh, hd]), op=ALU.mult),
            r=['tn', 'pvb'], w=[outkey])

    def phase1(self, l, Xsrc):
        nc, sc, S = self.nc, self.sc, self.S
        P, PB = self.P, self.PB
        with ExitStack() as es:
            def sb(name, shape, dt):
                return es.enter_context(nc.sbuf_tensor("p1_%d_" % l + name, list(shape), dt))
            pvb = sb("pvb", [128, NPV], F32)
            gs = sb("gs", [128, 64 * 3 + 96], F32)
            T = dict(sq=sb("sq", [128, 1024], F32), ss=sb("ss", [128, 8], F32), ss2=sb("ss2", [128, 8], F32),
                     rs=sb("rs", [128, 8], F32), tn=sb("tn", [128, 768], F32), eps=sb("eps", [128, 1], F32))
            xts = [sb("xt%d" % i, [128, 1024], F32) for i in range(2)]
            xn = [sb("xn%d" % i, [128, 1024], BF16) for i in range(2)]
            xnT = sb("xnT", [128, 8, TG], BF16)
            wst = [sb("wst%d" % i, [128, 8, 512], F32) for i in range(2)]
            wbf = [sb("wbf%d" % i, [128, 8, 512], BF16) for i in range(2)]
            fmst = [sb("fmst%d" % i, [128, TG], BF16) for i in range(2)]
            stg = [sb("stg%d" % i, [128, 4, TG], BF16) for i in range(1)]
            ob = [sb("ob%d" % i, [128, 512], BF16) for i in range(2)]
            vb = [sb("vb%d" % i, [128, 512], BF16) for i in range(2)]
            wqb_f = sb("wqb_f", [128, 2, 768], F32)
            wqb_b = sb("wqb_b", [128, 2, 768], BF16)
            wkvb_f = sb("wkvb_f", [128, 1024], F32)
            wkvb_b = sb("wkvb_b", [128, 1024], BF16)
            cqn = sb("cqn", [128, 256], BF16)
            ckvn = sb("ckvn", [128, 128], BF16)
            cqT = sb("cqT", [128, 2, 128], BF16)
            ckvT = sb("ckvT", [128, 128], BF16)
            qcf = sb("qcf", [128, 768], F32)
            qcb = sb("qcb", [128, 768], BF16)
            kcb = sb("kcb", [128, 768], BF16)
            krp = sb("krp", [128, 32], F32)
            krr = sb("krr", [128, 32], F32)
            rtmp = sb("rtmp", [128, 8, 16], F32)
            rtmp2 = sb("rtmp2", [128, 8, 16], F32)
            cs = sb("cs", [128, 32], F32)
            iwt = sb("iwt", [128, 8], F32)
            ssr = sb("ssr", [128, 1], F32)
            cstg_q = sb("cstg_q", [96, 8, 512], BF16)
            cstg_k = sb("cstg_k", [96, 8, 512], BF16)
            stg_dk = sb("stg_dk", [128, TG], BF16)
            stg_iq = sb("stg_iq", [128, 2, TG], BF16)
            stg_ik = sb("stg_ik", [128, TG], BF16)
            dkb = sb("dkb", [128, 128], BF16)
            iqb = sb("iqb", [128, 256], BF16)
            ikb = sb("ikb", [128, 128], BF16)

            sc.op('dve', lambda: nc.vector.memset(T['eps'][:], EPS), w=['eps'])
            sc.dma(pvb[:], self.pv[l].partition_broadcast(128), w=['pvb'])
            o = 1024
            g_aq, g_ak, g_bq, g_bk = (pvb[:, o + 64 * i: o + 64 * (i + 1)] for i in range(4))
            o += 256
            g_cq, g_ck = pvb[:, o:o + 96], pvb[:, o + 96:o + 192]
            o += 192
            g_dq, g_dk = pvb[:, o:o + 64], pvb[:, o + 64:o + 128]
            o += 128
            g_cql, g_ckvl = pvb[:, o:o + 256], pvb[:, o + 256:o + 384]
            for i, (g, s_) in enumerate(((g_aq, 0.125), (g_bq, 0.125), (g_dq, 0.125))):
                sc.op('dve', lambda g=g, s_=s_, i=i: nc.vector.tensor_scalar(
                    out=gs[:, 64 * i:64 * (i + 1)], in0=g, scalar1=s_, scalar2=None, op0=ALU.mult),
                    r=['pvb'], w=['pvb'])
            sc.op('dve', lambda: nc.vector.tensor_scalar(
                out=gs[:, 192:288], in0=g_cq, scalar1=96 ** -0.5, scalar2=None, op0=ALU.mult),
                r=['pvb'], w=['pvb'])
            gs_aq, gs_bq, gs_dq, gs_cq = gs[:, 0:64], gs[:, 64:128], gs[:, 128:192], gs[:, 192:288]
            sc.dma(wqb_f[:], self.wqb[l].rearrange("(c p) n -> p c n", p=128), w=['wqb_f'])
            sc.op('pool', lambda: nc.gpsimd.tensor_copy(out=wqb_b[:], in_=wqb_f[:]), r=['wqb_f'], w=['wqb_b'])
            sc.dma(wkvb_f[:], self.wkvb[l], w=['wkvb_f'])
            sc.op('pool', lambda: nc.gpsimd.tensor_copy(out=wkvb_b[:], in_=wkvb_f[:]), r=['wkvb_f'], w=['wkvb_b'])

            wv = self.w_in[l].rearrange("(c p) n -> p c n", p=128)
            wcnt = [0]

            def load_w(grp, ncols=512):
                i = wcnt[0] % 2
                wcnt[0] += 1
                sc.dma(wst[i][:, :, :ncols], wv[:, :, grp * 512: grp * 512 + ncols], w=['wst%d' % i])
                sc.op('pool', lambda: nc.gpsimd.tensor_copy(out=wbf[i][:, :, :ncols], in_=wst[i][:, :, :ncols]),
                      r=['wst%d' % i], w=['wbf%d' % i])
                return i

            GW = [512] * 27 + [128]
            for tg in range(self.NG):
                t0 = tg * TG
                for t in range(16):
                    xt, xnb = xts[t % 2], xn[t % 2]
                    kx, kn = 'xt%d' % (t % 2), 'xn%d' % (t % 2)
                    sc.dma(xt[:], Xsrc[t0 + t * 128: t0 + (t + 1) * 128, :], w=[kx])
                    sc.op('act', lambda xt=xt: nc.scalar.activation(
                        out=T['sq'][:], in_=xt[:], func=AF.Square, accum_out=T['ss'][:, 0:1]),
                        r=[kx], w=['sq', 'ss'])
                    sc.op('act', lambda: nc.scalar.activation(
                        out=T['ss2'][:, 0:1], in_=T['ss'][:, 0:1], func=AF.Sqrt, scale=1.0 / 1024,
                        bias=T['eps'][:, 0:1]), r=['ss', 'eps'], w=['ss2'])
                    sc.op('dve', lambda: nc.vector.reciprocal(out=T['rs'][:, 0:1], in_=T['ss2'][:, 0:1]),
                          r=['ss2'], w=['rs'])
                    sc.op('dve', lambda xt=xt, xnb=xnb: nc.vector.scalar_tensor_tensor(
                        out=xnb[:], in0=xt[:], scalar=T['rs'][:, 0:1], in1=pvb[:, 0:1024],
                        op0=ALU.mult, op1=ALU.mult), r=[kx, 'rs', 'pvb'], w=[kn])
                    pb = PB[t % 2]
                    kp = 'PB%d' % (t % 2)
                    for c in range(8):
                        sc.op('pe', lambda c=c, pb=pb, xnb=xnb: nc.tensor.transpose(
                            out=pb[:, c * 128:(c + 1) * 128], in_=xnb[:, c * 128:(c + 1) * 128],
                            identity=self.ident[:]), r=[kn, 'ident'], w=[kp], sig=(c == 7))
                    sc.op('act', lambda pb=pb, t=t: nc.scalar.copy(
                        out=xnT[:, :, t * 128:(t + 1) * 128],
                        in_=pb[:, :].rearrange("p (c t) -> p c t", c=8)), r=[kp], w=['xnT'])
                cur = load_w(0)
                for grp in range(NCOLG):
                    nxt = load_w(grp + 1, GW[grp + 1]) if grp + 1 < NCOLG else None
                    wb = wbf[cur]
                    kw = 'wbf%d' % cur
                    if grp < 12:
                        func = AF.Silu if grp < 4 else AF.Sigmoid
                        for sub in range(4):
                            st = fmst[sub % 2]
                            ks = 'fmst%d' % (sub % 2)
                            for tq in range(4):
                                pi = (sub * 4 + tq) % 2
                                ps = P[pi]
                                for c in range(8):
                                    sc.op('pe', lambda c=c, ps=ps, wb=wb, sub=sub, tq=tq: nc.tensor.matmul(
                                        ps[:, :], lhsT=wb[:, c, sub * 128:(sub + 1) * 128],
                                        rhs=xnT[:, c, tq * 512:(tq + 1) * 512], start=(c == 0), stop=(c == 7)),
                                        r=[kw, 'xnT'], w=['P%d' % pi], sig=(c == 7))
                                sc.op('act', lambda ps=ps, st=st, tq=tq, func=func: nc.scalar.activation(
                                    out=st[:, tq * 512:(tq + 1) * 512], in_=ps[:, :], func=func),
                                    r=['P%d' % pi], w=[ks])
                            row = (grp * 4 + sub) * 128
                            sc.dma(self.ZGT[row:row + 128, t0:t0 + TG], st[:], r=[ks], w=[('ZGT', grp, sub, tg)])
                    else:
                        for t in range(16):
                            pi = 2 + (t % 2)
                            ps = P[pi]
                            kps = 'P%d' % pi
                            ncol = GW[grp]
                            for c in range(8):
                                sc.op('pe', lambda c=c, ps=ps, wb=wb, t=t, ncol=ncol: nc.tensor.matmul(
                                    ps[:, :ncol], lhsT=xnT[:, c, t * 128:(t + 1) * 128], rhs=wb[:, c, :ncol],
                                    start=(c == 0), stop=(c == 7)), r=[kw, 'xnT'], w=[kps], sig=(c == 7))
                            self._p1_post(l, grp, tg, t, ps, kps, locals())
                    cur = nxt

    def _evac_T(self, pb, kp, dst, kdst, t, d, npair=4, eng='act'):
        nc, sc = self.nc, self.sc
        nj = 128 // d
        if npair > 1:
            src = pb[:, 0:npair * 128].rearrange("q (p jj r) -> q p r jj", p=npair, r=d)
            dd = dst.rearrange("q p (r m) -> q p r m", r=d)[:, :, :, nj * t: nj * t + nj]
        else:
            src = pb[:, 0:128].rearrange("q (jj r) -> q r jj", r=d)
            dd = dst.rearrange("q (r m) -> q r m", r=d)[:, :, nj * t: nj * t + nj]
        if eng == 'act':
            sc.op('act', lambda: nc.scalar.copy(out=dd, in_=src), r=[kp], w=[kdst])
        else:
            sc.op('dve', lambda: nc.vector.tensor_copy(out=dd, in_=src), r=[kp], w=[kdst])

    def _p1_post(self, l, grp, tg, t, ps, kps, L_):
        nc, sc, S = self.nc, self.sc, self.S
        PB = self.PB
        T = L_['T']
        t0 = tg * TG
        tok0 = t0 + t * 128
        ident = self.ident
        qk_groups = {12: ('aq', 1), 13: ('ak', 1), 15: ('bq', 1), 16: ('bq', 4), 17: ('bq', 16),
                     18: ('bk', 1), 19: ('bk', 4), 20: ('bk', 16), 24: ('dq', 1)}
        v_groups = {14: (self.V_A, 1), 21: (self.V_B[0], 1), 22: (self.V_B[1], 4), 23: (self.V_B[2], 16)}
        if grp in qk_groups:
            kind, d = qk_groups[grp]
            gain = dict(aq=L_['gs_aq'], ak=L_['g_ak'], bq=L_['gs_bq'], bk=L_['g_bk'], dq=L_['gs_dq'])[kind]
            dst = {12: self.QT_A, 13: self.KT_A, 15: self.QT_B[0], 16: self.QT_B[1], 17: self.QT_B[2],
                   18: self.KT_B[0], 19: self.KT_B[1], 20: self.KT_B[2], 24: self.QT_D}[grp]
            ob = L_['ob'][t % 2]
            ko = 'ob%d' % (t % 2)
            self.headnorm(T, ps[:, :], kps, 8, 64, gain, ob[:, :], ko)
            pb = PB[t % 2]
            kp = 'PB%d' % (t % 2)
            for p in range(4):
                sc.op('pe', lambda p=p: nc.tensor.transpose(
                    out=pb[:, p * 128:(p + 1) * 128], in_=ob[:, p * 128:(p + 1) * 128], identity=ident[:]),
                    r=[ko, 'ident'], w=[kp], sig=(p == 3))
            si = 0
            st = L_['stg'][si]
            self._evac_T(pb, kp, st[:], 'stg%d' % si, t, d)
            if t == 15:
                for p_ in range(4):
                    sc.dma(dst[p_ * 128:(p_ + 1) * 128, t0:t0 + TG], st[:, p_, :], r=['stg%d' % si], w=[(grp, tg, p_)])
        elif grp in v_groups:
            dst, d = v_groups[grp]
            vb = L_['vb'][t % 2]
            kv = 'vb%d' % (t % 2)
            sc.op('act', lambda: nc.scalar.copy(out=vb[:, :], in_=ps[:, :]), r=[kps], w=[kv])
            nj = 128 // d
            if d == 1:
                sc.dma(dst[tok0:tok0 + 128, :], vb[:, :], r=[kv], w=[(grp, tg, t)])
            else:
                dv = dst[t0:t0 + TG, :].rearrange("(r m) c -> m r c", r=d)[nj * t: nj * t + nj, :, :]
                sc.dma(dv, vb[:, :], r=[kv], w=[(grp, tg, t)])
        elif grp == 25:
            dkb, iqb, vb = L_['dkb'], L_['iqb'], L_['vb'][t % 2]
            kv = 'vb%d' % (t % 2)
            self.headnorm(T, ps[:, 0:128], kps, 2, 64, L_['g_dk'], dkb[:, :], 'dkb')
            sc.op('act', lambda: nc.scalar.copy(out=vb[:, 0:128], in_=ps[:, 128:256]), r=[kps], w=[kv])
            sc.dma(self.V_D[tok0:tok0 + 128, :], vb[:, 0:128], r=[kv], w=[('vd', tg, t)])
            sc.op('act', lambda: nc.scalar.copy(out=iqb[:, :], in_=ps[:, 256:512]), r=[kps], w=['iqb'])
            pb = PB[t % 2]
            kp = 'PB%d' % (t % 2)
            sc.op('pe', lambda: nc.tensor.transpose(out=pb[:, 0:128], in_=dkb[:, :], identity=ident[:]),
                  r=['dkb', 'ident'], w=[kp], sig=False)
            for p in range(2):
                sc.op('pe', lambda p=p: nc.tensor.transpose(
                    out=pb[:, (p + 1) * 128:(p + 2) * 128], in_=iqb[:, p * 128:(p + 1) * 128], identity=ident[:]),
                    r=['iqb', 'ident'], w=[kp], sig=(p == 1))
            sc.op('act', lambda: nc.scalar.copy(out=L_['stg_dk'][:, t * 128:(t + 1) * 128], in_=pb[:, 0:128]),
                  r=[kp], w=['stg_dk'])
            sc.op('act', lambda: nc.scalar.copy(
                out=L_['stg_iq'][:, :, t * 128:(t + 1) * 128],
                in_=pb[:, 128:384].rearrange("q (p t) -> q p t", p=2)), r=[kp], w=['stg_iq'])
            if t == 15:
                sc.dma(self.KT_D[:, t0:t0 + TG], L_['stg_dk'][:], r=['stg_dk'], w=[('ktd', tg)])
                for p_ in range(2):
                    sc.dma(self.IQT[p_ * 128:(p_ + 1) * 128, t0:t0 + TG], L_['stg_iq'][:, p_, :],
                           r=['stg_iq'], w=[('iqt', tg, p_)])
        elif grp == 27:
            ikb = L_['ikb']
            sc.op('act', lambda: nc.scalar.copy(out=ikb[:, :], in_=ps[:, 0:128]), r=[kps], w=['ikb'])
            pb = PB[t % 2]
            kp = 'PB%d' % (t % 2)
            sc.op('pe', lambda: nc.tensor.transpose(out=pb[:, 0:128], in_=ikb[:, :], identity=ident[:]),
                  r=['ikb', 'ident'], w=[kp])
            sc.op('act', lambda: nc.scalar.copy(out=L_['stg_ik'][:, t * 128:(t + 1) * 128], in_=pb[:, 0:128]),
                  r=[kp], w=['stg_ik'])
            if t == 15:
                sc.dma(self.IKT[:, t0:t0 + TG], L_['stg_ik'][:], r=['stg_ik'], w=[('ikt', tg)])
        elif grp == 26:
            self._p1_mla(l, tg, t, ps, kps, L_)

    def _p1_mla(self, l, tg, t, ps, kps, L_):
        nc, sc = self.nc, self.sc
        P, PB = self.P, self.PB
        T = L_['T']
        ident = self.ident
        t0 = tg * TG
        tok0 = t0 + t * 128
        cqn, ckvn, cqT, ckvT = L_['cqn'], L_['ckvn'], L_['cqT'], L_['ckvT']
        qcf, qcb, kcb, krp, krr = L_['qcf'], L_['qcb'], L_['kcb'], L_['krp'], L_['krr']
        rtmp, rtmp2, cs, iwt, ssr = L_['rtmp'], L_['rtmp2'], L_['cs'], L_['iwt'], L_['ssr']
        sc.op('act', lambda: nc.scalar.copy(out=iwt[:, :], in_=ps[:, 416:424]), r=[kps], w=['iwt'])
        sc.dma(self.IW[tok0:tok0 + 128, :], iwt[:, :], r=['iwt'], w=[('iw', tg, t)])
        sc.dma(cs[:, :], self.rope[tok0:tok0 + 128, :], w=['cs'])
        self.headnorm(T, ps[:, 0:256], kps, 1, 256, L_['g_cql'], cqn[:, :], 'cqn')
        self.headnorm(T, ps[:, 256:384], kps, 1, 128, L_['g_ckvl'], ckvn[:, :], 'ckvn')
        sc.op('act', lambda: nc.scalar.activation(out=krp[:, :], in_=ps[:, 384:416], func=AF.Square,
                                                   accum_out=ssr[:, 0:1]), r=[kps], w=['krp', 'ssr'])
        sc.op('dve', lambda: nc.vector.tensor_tensor(out=krp[:, :], in0=ps[:, 384:416], in1=L_['g_ck'][:, 64:96],
                                                     op=ALU.mult), r=[kps, 'pvb', 'krp'], w=['krp'])
        pb = PB[t % 2]
        kp = 'PB%d' % (t % 2)
        for p in range(2):
            sc.op('pe', lambda p=p: nc.tensor.transpose(out=pb[:, p * 128:(p + 1) * 128],
                                                        in_=cqn[:, p * 128:(p + 1) * 128], identity=ident[:]),
                  r=['cqn', 'ident'], w=[kp], sig=False)
        sc.op('pe', lambda: nc.tensor.transpose(out=pb[:, 256:384], in_=ckvn[:, :], identity=ident[:]),
              r=['ckvn', 'ident'], w=[kp])
        sc.op('act', lambda: nc.scalar.copy(out=cqT[:, :, :], in_=pb[:, 0:256].rearrange("q (p t) -> q p t", p=2)),
              r=[kp], w=['cqT'])
        sc.op('act', lambda: nc.scalar.copy(out=ckvT[:, :], in_=pb[:, 256:384]), r=[kp], w=['ckvT'])
        for half, (pp, c0, c1) in enumerate(((P[4], 0, 512), (P[5], 512, 768))):
            for c in range(2):
                sc.op('pe', lambda c=c, pp=pp, c0=c0, c1=c1: nc.tensor.matmul(
                    pp[:, 0:c1 - c0], lhsT=cqT[:, c, :], rhs=L_['wqb_b'][:, c, c0:c1], start=(c == 0), stop=(c == 1)),
                    r=['cqT', 'wqb_b'], w=['P%d' % (4 + half)], sig=(c == 1))
            sc.op('act', lambda pp=pp, c0=c0, c1=c1: nc.scalar.copy(out=qcf[:, c0:c1], in_=pp[:, 0:c1 - c0]),
                  r=['P%d' % (4 + half)], w=['qcf'])
        self.headnorm(T, qcf[:, :], 'qcf', 8, 96, L_['gs_cq'], qcf[:, :], 'qcf')
        self._rope(qcf, 'qcf', qcb, 'qcb', cs, rtmp, rtmp2)
        for half in range(2):
            sc.op('pe', lambda half=half: nc.tensor.matmul(
                P[4 + half][:, :], lhsT=ckvT[:, :], rhs=L_['wkvb_b'][:, half * 512:(half + 1) * 512],
                start=True, stop=True), r=['ckvT', 'wkvb_b'], w=['P%d' % (4 + half)])
        vb = L_['vb'][t % 2]
        kv = 'vb%d' % (t % 2)
        sc.op('act', lambda: nc.scalar.copy(out=vb[:, :], in_=P[5][:, :]), r=['P5'], w=[kv])
        sc.dma(self.V_C[tok0:tok0 + 128, :], vb[:, :], r=[kv], w=[('vc', tg, t)])
        sq, ss, ss2, rs = T['sq'], T['ss'], T['ss2'], T['rs']
        sc.op('act', lambda: nc.scalar.activation(out=sq[:, :512], in_=P[4][:, :], func=AF.Square),
              r=['P4'], w=['sq'])
        sc.op('dve', lambda: nc.vector.tensor_reduce(out=ss[:, :8], in_=sq[:, :512].rearrange("p (h d) -> p h d", d=64),
                                                     axis=AX.X, op=ALU.add), r=['sq'], w=['ss'])
        sc.op('dve', lambda: nc.vector.tensor_scalar(out=ss[:, :8], in0=ss[:, :8], scalar1=ssr[:, 0:1], scalar2=None,
                                                     op0=ALU.add), r=['ss', 'ssr'], w=['ss'])
        sc.op('act', lambda: nc.scalar.activation(out=ss2[:, :8], in_=ss[:, :8], func=AF.Sqrt, scale=1.0 / 96,
                                                   bias=T['eps'][:, 0:1]), r=['ss', 'eps'], w=['ss2'])
        sc.op('dve', lambda: nc.vector.reciprocal(out=rs[:, :8], in_=ss2[:, :8]), r=['ss2'], w=['rs'])
        kv3 = qcf[:, :].rearrange("p (h d) -> p h d", d=96)
        sc.op('dve', lambda: nc.vector.tensor_tensor(
            out=kv3[:, :, 0:64], in0=P[4][:, :].rearrange("p (h d) -> p h d", d=64),
            in1=rs[:, :8].unsqueeze(2).broadcast_to([128, 8, 64]), op=ALU.mult), r=['P4', 'rs', 'qcb'], w=['qcf'])
        sc.op('dve', lambda: nc.vector.tensor_tensor(
            out=kv3[:, :, 0:64], in0=kv3[:, :, 0:64],
            in1=L_['g_ck'][:, 0:64].unsqueeze(1).broadcast_to([128, 8, 64]), op=ALU.mult),
            r=['qcf', 'pvb'], w=['qcf'])
        sc.op('dve', lambda: nc.vector.tensor_tensor(
            out=kv3[:, :, 64:96], in0=krp[:, :].unsqueeze(1).broadcast_to([128, 8, 32]),
            in1=rs[:, :8].unsqueeze(2).broadcast_to([128, 8, 32]), op=ALU.mult), r=['krp', 'rs', 'qcf'], w=['qcf'])
        self._rope(qcf, 'qcf', kcb, 'kcb', cs, rtmp, rtmp2)
        for which, (src, ksrc, stg_, kst, dst) in enumerate(((qcb, 'qcb', L_['cstg_q'], 'cstg_q', self.QT_C),
                                                            (kcb, 'kcb', L_['cstg_k'], 'cstg_k', self.KT_C))):
            pb = PB[which]
            kp = 'PB%d' % which
            for h in range(8):
                sc.op('pe', lambda h=h, pb=pb, src=src: nc.tensor.transpose(
                    out=pb[0:96, h * 128:(h + 1) * 128], in_=src[:, h * 96:(h + 1) * 96], identity=ident[:]),
                    r=[ksrc, 'ident'], w=[kp], sig=(h == 7))
            tt = t % 4
            sc.op('act', lambda pb=pb, stg_=stg_, tt=tt: nc.scalar.copy(
                out=stg_[:, :, tt * 128:(tt + 1) * 128], in_=pb[0:96, :].rearrange("q (h t) -> q h t", h=8)),
                r=[kp], w=[kst])
            if tt == 3:
                c0 = tok0 - 384
                for h_ in range(8):
                    sc.dma(dst[h_][:, c0:c0 + 512], stg_[:, h_, :], r=[kst], w=[(kst, tg, t, h_)])

    def _rope(self, src, ksrc, dst, kdst, cs, rtmp, rtmp2):
        nc, sc = self.nc, self.sc
        s3 = src[:, :].rearrange("p (h d) -> p h d", d=96)
        d3 = dst[:, :].rearrange("p (h d) -> p h d", d=96)
        cosb = cs[:, 0:16].unsqueeze(1).broadcast_to([128, 8, 16])
        sinb = cs[:, 16:32].unsqueeze(1).broadcast_to([128, 8, 16])
        x1, x2 = s3[:, :, 64:80], s3[:, :, 80:96]
        tt = nc.vector.tensor_tensor
        sc.op('act', lambda: nc.scalar.copy(out=d3[:, :, 0:64], in_=s3[:, :, 0:64]), r=[ksrc], w=[kdst])
        sc.op('dve', lambda: tt(out=rtmp[:], in0=x1, in1=cosb, op=ALU.mult), r=[ksrc, 'cs'], w=['rtmp'])
        sc.op('dve', lambda: tt(out=rtmp2[:], in0=x2, in1=sinb, op=ALU.mult), r=[ksrc, 'cs'], w=['rtmp2'])
        sc.op('dve', lambda: tt(out=d3[:, :, 64:80], in0=rtmp[:], in1=rtmp2[:], op=ALU.subtract),
              r=['rtmp', 'rtmp2', kdst], w=[kdst])
        sc.op('dve', lambda: tt(out=rtmp[:], in0=x2, in1=cosb, op=ALU.mult), r=[ksrc, 'cs', kdst], w=['rtmp'])
        sc.op('dve', lambda: tt(out=rtmp2[:], in0=x1, in1=sinb, op=ALU.mult), r=[ksrc, 'cs', kdst], w=['rtmp2'])
        sc.op('dve', lambda: tt(out=d3[:, :, 80:96], in0=rtmp[:], in1=rtmp2[:], op=ALU.add),
              r=['rtmp', 'rtmp2', kdst], w=[kdst])

    def wload(self, dst, src, ncols, key, step=2048):
        for c0 in range(0, ncols, step):
            c1 = min(ncols, c0 + step)
            self.sc.dma(dst[:, c0:c1], src[:, c0:c1], w=[key])

    def vload(self, dst_of, src_of, nblk, key, step=16):
        for b0 in range(0, nblk, step):
            b1 = min(nblk, b0 + step)
            self.sc.dma(dst_of(b0, b1), src_of(b0, b1), w=[key])

    def _alloc(self, es, prefix):
        nc = self.nc

        def sb(name, shape, dt):
            return es.enter_context(nc.sbuf_tensor(prefix + name, list(shape), dt))
        return sb

    def _fin_bufs(self, sb):
        return dict(o=sb("fo", [65, 512], F32), rd=sb("frd", [65, 512], F32), yb=[sb("fyb%d" % i, [64, 512], BF16) for i in range(2)],
                    cnt=[0])

    def finish_head(self, F, src, ksrc, m, row0, q0, n=512, sink=None):
        nc, sc = self.nc, self.sc
        o, rd = F['o'], F['rd']
        i = F['cnt'][0] % 2
        F['cnt'][0] += 1
        yb = F['yb'][i]
        ky = 'fyb%d' % i
        sc.op('act', lambda: nc.scalar.copy(out=o[0:65, :n], in_=src), r=[ksrc], w=['fo'])
        if sink is not None:
            sc.op('dve', lambda: nc.vector.tensor_scalar(out=rd[64:65, :n], in0=o[64:65, :n], scalar1=sink, scalar2=None,
                                                         op0=ALU.add), r=['fo', 'esk'], w=['frd'])
            sc.op('dve', lambda: nc.vector.reciprocal(out=rd[64:65, :n], in_=rd[64:65, :n]), r=['frd'], w=['frd'])
        else:
            sc.op('dve', lambda: nc.vector.reciprocal(out=rd[64:65, :n], in_=o[64:65, :n]), r=['fo'], w=['frd'])
        pb = self.P[0]
        sc.op('pe', lambda: nc.tensor.matmul(pb[0:64, :n], lhsT=self.ones_f[64:65, 0:64], rhs=rd[64:65, :n],
                                             start=True, stop=True), r=['frd', 'ones_f'], w=['P0'])
        sc.op('dve', lambda: nc.vector.tensor_tensor(out=yb[:, :n], in0=o[0:64, :n], in1=pb[0:64, :n], op=ALU.mult),
              r=['fo', 'P0'], w=[ky])
        sc.dma(self.YT[m][row0:row0 + 64, q0:q0 + n], yb[:, :n], r=[ky], w=[('yt', m, row0, q0)])

    def band_batch(self, qt_of, kt_of, va_of, bias_of, blocks, OT, kOT, psb, kpsb, pts, kpts, rkeys):
        nc, sc = self.nc, self.sc
        ident = self.ident
        for j, has_prev in enumerate(blocks):
            cs = slice(j * 128, (j + 1) * 128)
            for pc in (0, 1):
                if pc == 0 and not has_prev:
                    continue
                ps = psb[pc]
                sc.op('pe', lambda ps=ps, j=j, pc=pc, cs=cs: nc.tensor.matmul(
                    ps[:, cs], lhsT=kt_of(j, pc), rhs=qt_of(j), start=True, stop=False),
                    r=rkeys, w=[kpsb[pc]], sig=False)
                sc.op('pe', lambda ps=ps, pc=pc, cs=cs: nc.tensor.matmul(
                    ps[:, cs], lhsT=ident[:, :], rhs=bias_of(pc), start=False, stop=True),
                    r=rkeys + ['ident'], w=[kpsb[pc]])
        for j, has_prev in enumerate(blocks):
            cs = slice(j * 128, (j + 1) * 128)
            for pc in (0, 1):
                if pc == 0 and not has_prev:
                    continue
                sc.op('act', lambda pc=pc, cs=cs: nc.scalar.activation(out=pts[pc][:, cs], in_=psb[pc][:, cs], func=AF.Exp),
                      r=[kpsb[pc]], w=[kpts[pc]])
        for j, has_prev in enumerate(blocks):
            cs = slice(j * 128, (j + 1) * 128)
            first = True
            for pc in (0, 1):
                if pc == 0 and not has_prev:
                    continue
                sc.op('pe', lambda pc=pc, cs=cs, j=j, first=first: nc.tensor.matmul(
                    OT[0:65, cs], lhsT=va_of(j, pc), rhs=pts[pc][:, cs], start=first, stop=(pc == 1)),
                    r=rkeys + [kpts[pc]], w=[kOT], sig=(pc == 1))
                first = False

    def phaseC(self, l):
        nc, sc, S = self.nc, self.sc, self.S
        P = self.P
        NT = self.NT
        with ExitStack() as es:
            sb = self._alloc(es, "pc%d_" % l)
            kt = [sb("kt%d" % i, [96, S], BF16) for i in range(2)]
            va = [sb("va%d" % i, [128, NT, 65], BF16) for i in range(2)]
            qt = [sb("qt%d" % i, [96, 512], BF16) for i in range(2)]
            pt = [sb("pt%d" % i, [128, 512], BF16) for i in range(2)]
            F = self._fin_bufs(sb)
            for i in range(2):
                sc.op('dve', lambda i=i: nc.vector.memset(va[i][:, :, 64:65], 1.0), w=['va%d' % i])
            cnt = 0
            for h in range(8):
                i = h % 2
                self.wload(kt[i], self.KT_C[h], S, 'kt%d' % i)
                vsrc = self.V_C.rearrange("(n p) c -> p n c", p=128)
                self.vload(lambda a, b_, i=i: va[i][:, a:b_, 0:64], lambda a, b_, h=h: vsrc[:, a:b_, h * 64:(h + 1) * 64],
                           NT, 'va%d' % i)
                for qg in range(S // 512):
                    qi = cnt % 2
                    cnt += 1
                    sc.dma(qt[qi][:, :], self.QT_C[h][:, qg * 512:(qg + 1) * 512], w=['qt%d' % qi])
                    OT, kOT = P[4 + qi], 'P%d' % (4 + qi)
                    nkb = 4 * qg + 4
                    for kb in range(nkb):
                        m = max(0, kb - 4 * qg)
                        c0 = 128 * m
                        pi = 2 + kb % 2
                        ps, kps = P[pi], 'P%d' % pi
                        diag = kb >= 4 * qg
                        sc.op('pe', lambda ps=ps, kb=kb, c0=c0, diag=diag: nc.tensor.matmul(
                            ps[:, c0:512], lhsT=kt[i][:, kb * 128:(kb + 1) * 128], rhs=qt[qi][:, c0:512],
                            start=True, stop=not diag), r=['kt%d' % i, 'qt%d' % qi], w=[kps], sig=not diag)
                        if diag:
                            sc.op('pe', lambda ps=ps, c0=c0: nc.tensor.matmul(
                                ps[:, c0:c0 + 128], lhsT=self.ident[:, :], rhs=self.caus_kq[:, :], start=False, stop=True),
                                r=['ident', 'caus_kq'], w=[kps])
                        pti = kb % 2
                        sc.op('act', lambda ps=ps, c0=c0, pti=pti: nc.scalar.activation(
                            out=pt[pti][:, c0:512], in_=ps[:, c0:512], func=AF.Exp), r=[kps], w=['pt%d' % pti])
                        sc.op('pe', lambda kb=kb, c0=c0, pti=pti, OT=OT, nkb=nkb: nc.tensor.matmul(
                            OT[0:65, c0:512], lhsT=va[i][:, kb, :], rhs=pt[pti][:, c0:512],
                            start=(kb == 0), stop=(kb == nkb - 1)), r=['va%d' % i, 'pt%d' % pti], w=[kOT],
                            sig=(kb == nkb - 1))
                    self.finish_head(F, OT[0:65, :], kOT, 2, h * 64, qg * 512)
        sc.barrier()

    def phaseD(self, l):
        nc, sc, S = self.nc, self.sc, self.S
        P = self.P
        NT = self.NT
        HO = (0, 4, 1, 5, 2, 6, 3, 7)
        with ExitStack() as es:
            sb = self._alloc(es, "pd%d_" % l)
            kt = sb("kt", [128, S], BF16)
            va = sb("va", [128, NT, 2, 65], BF16)
            bst = sb("bst", [128, 8 * 2 * 128], F32)
            bias = sb("bias", [128, 8, 2, 128], BF16)
            qt = [sb("qt%d" % i, [128, 4, 512], BF16) for i in range(2)]
            pts = [[sb("pt%d%d" % (a, b), [128, 512], BF16) for b in range(2)] for a in range(2)]
            snk = sb("snk", [65, 8], F32)
            esk = sb("esk", [65, 8], F32)
            F = self._fin_bufs(sb)
            sc.op('dve', lambda: nc.vector.memset(va[:, :, :, 64:65], 1.0), w=['va'])
            self.wload(kt, self.KT_D, S, 'kt')
            vsrc = self.V_D.rearrange("(n p) c -> p n c", p=128)
            for g_ in range(2):
                self.vload(lambda a, b_, g_=g_: va[:, a:b_, g_, 0:64], lambda a, b_, g_=g_: vsrc[:, a:b_, g_ * 64:(g_ + 1) * 64],
                           NT, 'va')
            sc.dma(bst[:, :], self.biasD[:, :], w=['bst'])
            sc.op('pool', lambda: nc.gpsimd.tensor_copy(out=bias[:].rearrange("k h p q -> k (h p q)"), in_=bst[:, :]),
                  r=['bst'], w=['bias'])
            sc.dma(snk[64:65, :], self.pv[l:l + 1, NPV - 8:NPV], w=['snk'])
            sc.op('act', lambda: nc.scalar.activation(out=esk[64:65, :], in_=snk[64:65, :], func=AF.Exp),
                  r=['snk'], w=['esk'])
            for qg in range(S // 512):
                qi = qg % 2
                for p_ in range(4):
                    sc.dma(qt[qi][:, p_, :], self.QT_D[p_ * 128:(p_ + 1) * 128, qg * 512:(qg + 1) * 512], w=['qt%d' % qi])
                for r in range(8):
                    h, p, half = HO[r], r // 2, r % 2
                    b = 64 * half
                    a = r % 2
                    OT, kOT = P[4 + a], 'P%d' % (4 + a)
                    psb = (P[2 * a], P[2 * a + 1])
                    kpsb = ('P%d' % (2 * a), 'P%d' % (2 * a + 1))
                    blocks = [(4 * qg + j) > 0 for j in range(4)]
                    self.band_batch(
                        qt_of=lambda j: qt[qi][b:b + 64, p, j * 128:(j + 1) * 128],
                        kt_of=lambda j, pc: kt[b:b + 64, (4 * qg + j - 1 + pc) * 128:(4 * qg + j + pc) * 128],
                        va_of=lambda j, pc: va[:, 4 * qg + j - 1 + pc, half, :],
                        bias_of=lambda pc: bias[:, h, pc, :],
                        blocks=blocks, OT=OT, kOT=kOT, psb=psb, kpsb=kpsb, pts=pts[a],
                        kpts=('pt%d0' % a, 'pt%d1' % a), rkeys=['kt', 'va', 'bias', 'qt%d' % qi])
                    self.finish_head(F, OT[0:65, :], kOT, 3, h * 64, qg * 512, sink=esk[64:65, h:h + 1])
        sc.barrier()

    def phaseB(self, l):
        nc, sc, S = self.nc, self.sc, self.S
        P = self.P
        with ExitStack() as es:
            sb = self._alloc(es, "pb%d_" % l)
            ktw = sb("ktw", [128, 4, 2 * TG], BF16)
            vaw = sb("vaw", [128, 32, 8, 65], BF16)
            qt = sb("qt", [128, 4, TG], BF16)
            bst = sb("bst", [128, 8 * 2 * 128], F32)
            bias = sb("bias", [128, 8, 2, 128], BF16)
            acc = [sb("acc%d" % h, [65, TG], F32) for h in range(8)]
            pts = [[sb("pt%d%d" % (a, b), [128, 512], BF16) for b in range(2)] for a in range(2)]
            F = self._fin_bufs(sb)
            sc.op('dve', lambda: nc.vector.memset(vaw[:, :, :, 64:65], 1.0), w=['vaw'])
            for tg in range(self.NG):
                t0 = tg * TG
                for g, d in enumerate(B_DIL):
                    ktv = self.KT_B[g]
                    vv = self.V_B[g].rearrange("(n p) c -> p n c", p=128)
                    if tg > 0:
                        for p_ in range(4):
                            for hb in range(2):
                                sc.dma(ktw[:, p_, hb * TG:(hb + 1) * TG],
                                       ktv[p_ * 128:(p_ + 1) * 128, t0 - TG + hb * TG:t0 + hb * TG], w=['ktw'])
                        for h_ in range(8):
                            for hb in range(2):
                                sc.dma(vaw[:, 16 * hb:16 * hb + 16, h_, 0:64],
                                       vv[:, 16 * tg - 16 + 16 * hb:16 * tg + 16 * hb, h_ * 64:(h_ + 1) * 64], w=['vaw'])
                    else:
                        for p_ in range(4):
                            sc.dma(ktw[:, p_, TG:2 * TG], ktv[p_ * 128:(p_ + 1) * 128, 0:TG], w=['ktw'])
                        for h_ in range(8):
                            sc.dma(vaw[:, 16:32, h_, 0:64], vv[:, 0:16, h_ * 64:(h_ + 1) * 64], w=['vaw'])
                    for p_ in range(4):
                        sc.dma(qt[:, p_, :], self.QT_B[g][p_ * 128:(p_ + 1) * 128, t0:t0 + TG], w=['qt'])
                    if tg == 0:
                        sc.dma(bst[:, :], self.biasB[:, g * 2048:(g + 1) * 2048], w=['bst'])
                        sc.op('pool', lambda: nc.gpsimd.tensor_copy(out=bias[:].rearrange("k h p q -> k (h p q)"),
                                                                    in_=bst[:, :]), r=['bst'], w=['bias'])
                    elif g == 0 or True:
                        sc.dma(bst[:, :], self.biasB[:, g * 2048:(g + 1) * 2048], w=['bst'])
                        sc.op('pool', lambda: nc.gpsimd.tensor_copy(out=bias[:].rearrange("k h p q -> k (h p q)"),
                                                                    in_=bst[:, :]), r=['bst'], w=['bias'])

                    def prev_of(nl):
                        if d == 1:
                            return (16 * tg + nl) > 0, 16 + nl - 1
                        if d == 4:
                            if nl % 4 > 0:
                                return True, 16 + nl - 1
                            return tg > 0, 16 + nl - 13
                        return tg > 0, nl
                    for bb in range(4):
                        for h in range(8):
                            p, b = h // 2, 64 * (h % 2)
                            a = h % 2
                            OT, kOT = P[4 + a], 'P%d' % (4 + a)
                            psb = (P[2 * a], P[2 * a + 1])
                            kpsb = ('P%d' % (2 * a), 'P%d' % (2 * a + 1))
                            pv_ = [prev_of(4 * bb + j) for j in range(4)]
                            blocks = [x[0] for x in pv_]

                            def widx(j, pc):
                                return (16 + 4 * bb + j) if pc == 1 else pv_[j][1]
                            self.band_batch(
                                qt_of=lambda j: qt[b:b + 64, p, (4 * bb + j) * 128:(4 * bb + j + 1) * 128],
                                kt_of=lambda j, pc: ktw[b:b + 64, p, widx(j, pc) * 128:(widx(j, pc) + 1) * 128],
                                va_of=lambda j, pc: vaw[:, widx(j, pc), h, :],
                                bias_of=lambda pc: bias[:, h, pc, :],
                                blocks=blocks, OT=OT, kOT=kOT, psb=psb, kpsb=kpsb, pts=pts[a],
                                kpts=('pt%d0' % a, 'pt%d1' % a), rkeys=['ktw', 'vaw', 'bias', 'qt'])
                            ka = 'acc%d' % h
                            if d == 1:
                                sc.op('act', lambda h=h, bb=bb, OT=OT: nc.scalar.copy(
                                    out=acc[h][0:65, bb * 512:(bb + 1) * 512], in_=OT[0:65, :]), r=[kOT], w=[ka])
                            elif d == 4:
                                dst = acc[h][0:65, :].rearrange("c (i r) -> c r i", r=4)[:, bb, :]
                                sc.op('dve', lambda dst=dst, OT=OT: nc.vector.tensor_tensor(
                                    out=dst, in0=dst, in1=OT[0:65, :], op=ALU.add), r=[kOT, ka], w=[ka])
                            else:
                                dst = acc[h][0:65, :].rearrange("c (i r) -> c r i", r=16)[:, 4 * bb:4 * bb + 4, :]
                                sc.op('dve', lambda dst=dst, OT=OT: nc.vector.tensor_tensor(
                                    out=dst, in0=dst, in1=OT[0:65, :].rearrange("c (j i) -> c j i", j=4), op=ALU.add),
                                    r=[kOT, ka], w=[ka])
                for h in range(8):
                    for ch in range(4):
                        self.finish_head(F, acc[h][0:65, ch * 512:(ch + 1) * 512], 'acc%d' % h, 1, h * 64,
                                         t0 + ch * 512)
        sc.barrier()

    def phaseA(self, l):
        nc, sc, S = self.nc, self.sc, self.S
        P = self.P
        NT = self.NT
        ident = self.ident
        with ExitStack() as es:
            sb = self._alloc(es, "pa%d_" % l)
            ikT = sb("ikT", [128, S], BF16)
            IS = sb("IS", [128, S], F32)
            NM = sb("NM", [128, 4, S], BF16)
            iqT = sb("iqT", [128, 2, 512], BF16)
            iqx = sb("iqx", [32, 2, 512], BF16)
            iw = sb("iw", [128, 4, 8], F32)
            R = [sb("R%d" % i, [128, 512], F32) for i in range(2)]
            mabs, mid, cnt, tt_ = (sb(n_, [128, 1], F32) for n_ in ("mabs", "mid", "cnt", "tt"))
            H = sb("H", [128, 32], F32)
            Hn = sb("Hn", [128, 32], F32)
            thr = sb("thr", [128, 1], F32)
            kt = sb("kt", [128, S], BF16)
            va = sb("va", [128, NT, 2, 65], BF16)
            bst = sb("bst", [128, 13 * 128], F32)
            bias = sb("bias", [128, 8, 13, 128], BF16)
            b31 = sb("b31", [128, 8], F32)
            qt = sb("qt", [128, 512], BF16)
            pt = [sb("pt%d" % i, [128, 512], BF16) for i in range(2)]
            F = self._fin_bufs(sb)
            sc.op('dve', lambda: nc.vector.memset(va[:, :, :, 64:65], 1.0), w=['va'])
            self.wload(ikT, self.IKT, S, 'ikT')
            sc.dma(b31[:, :], self.b31[:, :], w=['b31'])
            for h in range(8):
                sc.dma(bst[:, :], self.biasA[h], w=['bst'])
                sc.op('pool', lambda h=h: nc.gpsimd.tensor_copy(out=bias[:, h].rearrange("k d q -> k (d q)"), in_=bst[:, :]),
                      r=['bst'], w=['bias'])
            for qg in range(S // 512):
                q0 = qg * 512
                for p_ in range(2):
                    sc.dma(iqT[:, p_, :], self.IQT[p_ * 128:(p_ + 1) * 128, q0:q0 + 512], w=['iqT'])
                    sc.dma(iqx[:, p_, :], self.IQT[p_ * 128 + 96:(p_ + 1) * 128, q0:q0 + 512], w=['iqT'])
                sc.dma(iw[:, :, :], self.IW[q0:q0 + 512, :].rearrange("(j p) e -> p j e", p=128), w=['iw'])
                for j in range(4):
                    qb = 4 * qg + j
                    N = (qb + 1) * 128
                    for kc in range((N + 511) // 512):
                        w_ = min(512, N - 512 * kc)
                        for h in range(8):
                            pi = h % 2
                            bp = 32 * (h % 4)
                            if bp == 96:
                                lhs_ = iqx[0:32, h // 4, j * 128:(j + 1) * 128]
                                bp = 0
                            else:
                                lhs_ = iqT[bp:bp + 32, h // 4, j * 128:(j + 1) * 128]
                            sc.op('pe', lambda pi=pi, bp=bp, h=h, kc=kc, w_=w_, j=j, lhs_=lhs_: nc.tensor.matmul(
                                P[pi][:, :w_], lhsT=lhs_,
                                rhs=ikT[bp:bp + 32, kc * 512:kc * 512 + w_], start=True, stop=True),
                                r=['iqT', 'ikT'], w=['P%d' % pi])
                            sc.op('act', lambda pi=pi, w_=w_: nc.scalar.activation(
                                out=R[pi][:, :w_], in_=P[pi][:, :w_], func=AF.Relu), r=['P%d' % pi], w=['R%d' % pi])
                            dst = IS[:, kc * 512:kc * 512 + w_]
                            if h == 0:
                                sc.op('dve', lambda pi=pi, w_=w_, dst=dst, j=j: nc.vector.tensor_scalar(
                                    out=dst, in0=R[pi][:, :w_], scalar1=iw[:, j, 0:1], scalar2=None, op0=ALU.mult),
                                    r=['R%d' % pi, 'iw'], w=['IS'])
                            else:
                                sc.op('dve', lambda pi=pi, w_=w_, dst=dst, j=j, h=h: nc.vector.scalar_tensor_tensor(
                                    out=dst, in0=R[pi][:, :w_], scalar=iw[:, j, h:h + 1], in1=dst, op0=ALU.mult,
                                    op1=ALU.add), r=['R%d' % pi, 'iw', 'IS'], w=['IS'])
                    sc.op('dve', lambda N=N: nc.vector.tensor_reduce(out=mabs[:, :], in_=IS[:, :N], axis=AX.X, op=ALU.max,
                                                                     apply_absolute_value=True), r=['IS'], w=['mabs'])
                    sc.op('dve', lambda qb=qb, N=N: nc.vector.tensor_tensor(
                        out=IS[:, qb * 128:N], in0=IS[:, qb * 128:N], in1=self.causneg, op=ALU.add),
                        r=['IS', 'cst_f'], w=['IS'])
                    sc.op('dve', lambda: nc.vector.tensor_scalar(out=H[:, :], in0=self.pow2, scalar1=mabs[:, 0:1],
                                                                 scalar2=1.0009765625, op0=ALU.mult, op1=ALU.mult),
                          r=['mabs', 'cst_f'], w=['H'])
                    sc.op('dve', lambda: nc.vector.tensor_scalar(out=Hn[:, :], in0=H[:, :], scalar1=-1.0, scalar2=None,
                                                                 op0=ALU.mult), r=['H'], w=['Hn'])
                    sc.op('dve', lambda: nc.vector.memset(mid[:, :], 0.0), w=['mid'])
                    for n in range(NBIS):
                        sc.op('dve', lambda N=N, j=j: nc.vector.tensor_scalar(
                            out=NM[:, j, :N], in0=IS[:, :N], scalar1=mid[:, 0:1], scalar2=None, op0=ALU.is_ge,
                            op1=ALU.add, accum_out=cnt[:, 0:1]), r=['IS', 'mid'], w=['NM%d' % j, 'cnt'])
                        sc.op('dve', lambda n=n: nc.vector.tensor_scalar(
                            out=tt_[:, :], in0=cnt[:, :], scalar1=TOPK - 0.5, scalar2=H[:, n:n + 1], op0=ALU.is_ge,
                            op1=ALU.mult), r=['cnt', 'H'], w=['tt'])
                        sc.op('dve', lambda n=n: nc.vector.scalar_tensor_tensor(
                            out=mid[:, :], in0=tt_[:, :], scalar=Hn[:, n + 1:n + 2], in1=mid[:, :], op0=ALU.add,
                            op1=ALU.add), r=['tt', 'Hn', 'mid'], w=['mid'])
                    sc.op('dve', lambda: nc.vector.tensor_tensor(out=thr[:, :], in0=mid[:, :], in1=Hn[:, NBIS:NBIS + 1],
                                                                 op=ALU.add), r=['mid', 'Hn'], w=['thr'])
                    if self.dbg:
                        sc.dma(self.THR[qb * 128:(qb + 1) * 128, :], thr[:, :], r=['thr'], w=[('thrd', qb)])
                    sc.op('dve', lambda N=N, j=j: nc.vector.tensor_scalar(
                        out=NM[:, j, :N], in0=IS[:, :N], scalar1=thr[:, 0:1], scalar2=NEG, op0=ALU.is_lt, op1=ALU.mult),
                        r=['IS', 'thr'], w=['NM%d' % j])
                nkb = 4 * qg + 4
                nmk = ['NM%d' % j for j in range(4)]
                for hp in range(4):
                    self.wload(kt, self.KT_A[hp * 128:(hp + 1) * 128, :], nkb * 128, 'kt')
                    vsrc = self.V_A.rearrange("(n p) c -> p n c", p=128)
                    for e_ in range(2):
                        hh_ = 2 * hp + e_
                        self.vload(lambda a, b_, e_=e_: va[:, a:b_, e_, 0:64],
                                   lambda a, b_, hh_=hh_: vsrc[:, a:b_, hh_ * 64:(hh_ + 1) * 64], nkb, 'va')
                    sc.dma(qt[:, :], self.QT_A[hp * 128:(hp + 1) * 128, q0:q0 + 512], w=['qt'])
                    for e in range(2):
                        h = 2 * hp + e
                        b = 64 * e
                        OT, kOT = P[4 + e], 'P%d' % (4 + e)
                        for kb in range(nkb):
                            m = max(0, kb - 4 * qg)
                            c0 = 128 * m
                            pi = 2 + kb % 2
                            ps, kps = P[pi], 'P%d' % pi
                            mm = []
                            mm.append((lambda s_, t_, ps=ps, kb=kb, c0=c0: nc.tensor.matmul(
                                ps[:, c0:512], lhsT=kt[b:b + 64, kb * 128:(kb + 1) * 128], rhs=qt[b:b + 64, c0:512],
                                start=s_, stop=t_), ['kt', 'qt']))
                            jsplit = 4
                            for jj in range(m, 4):
                                cs = slice(jj * 128, (jj + 1) * 128)
                                mm.append((lambda s_, t_, ps=ps, jj=jj, kb=kb, cs=cs: nc.tensor.matmul(
                                    ps[:, cs], lhsT=NM[:, jj, kb * 128:(kb + 1) * 128], rhs=ident[:, :], start=s_, stop=t_),
                                    ['NM%d' % jj, 'ident']))
                                D = 4 * qg + jj - kb
                                if D < 13:
                                    mm.append((lambda s_, t_, ps=ps, cs=cs, D=D, h=h: nc.tensor.matmul(
                                        ps[:, cs], lhsT=ident[:, :], rhs=bias[:, h, D, :], start=s_, stop=t_),
                                        ['bias', 'ident']))
                                elif jsplit == 4:
                                    jsplit = jj
                            for ii, (fn, rk) in enumerate(mm):
                                last = ii == len(mm) - 1
                                sc.op('pe', lambda fn=fn, ii=ii, last=last: fn(ii == 0, last), r=rk, w=[kps], sig=last)
                            pti = kb % 2
                            cn = 128 * jsplit
                            if cn > c0:
                                sc.op('act', lambda ps=ps, c0=c0, cn=cn, pti=pti: nc.scalar.activation(
                                    out=pt[pti][:, c0:cn], in_=ps[:, c0:cn], func=AF.Exp), r=[kps], w=['pt%d' % pti])
                            if cn < 512:
                                cf = max(cn, c0)
                                sc.op('act', lambda ps=ps, cf=cf, pti=pti, h=h: nc.scalar.activation(
                                    out=pt[pti][:, cf:512], in_=ps[:, cf:512], func=AF.Exp, bias=b31[:, h:h + 1]),
                                    r=[kps, 'b31'], w=['pt%d' % pti])
                            sc.op('pe', lambda kb=kb, c0=c0, pti=pti, OT=OT, e=e: nc.tensor.matmul(
                                OT[0:65, c0:512], lhsT=va[:, kb, e, :], rhs=pt[pti][:, c0:512],
                                start=(kb == 0), stop=(kb == nkb - 1)), r=['va', 'pt%d' % pti], w=[kOT],
                                sig=(kb == nkb - 1))
                        self.finish_head(F, OT[0:65, :], kOT, 0, h * 64, q0)
        sc.barrier()

    def phase3(self, l, Xsrc, Xdst, keep=(0, 1, 2, 3)):
        nc, sc, S = self.nc, self.sc, self.S
        P = self.P
        with ExitStack() as es:
            sb = self._alloc(es, "p3%d_" % l)
            wst = sb("wst", [128, 4, 1024], F32)
            wbr_b = sb("wbr_b", [128, 4, 4, 1024], BF16)
            wout_b = sb("wout_b", [128, 8, 1024], BF16)
            yt = sb("yt", [128, 4, 4, 512], BF16)
            zt = sb("zt", [128, 4, 4, 512], BF16)
            gt = sb("gt", [128, 32, 512], BF16)
            mT = sb("mT", [128, 8, 512], BF16)
            tmp = [sb("tmp%d" % i, [128, 512], F32) for i in range(2)]
            macc = sb("macc", [128, 512], F32)
            xt = [sb("xt%d" % i, [128, 1024], F32) for i in range(2)]
            xo = [sb("xo%d" % i, [128, 1024], F32) for i in range(2)]
            for b in range(4):
                sc.dma(wst[:, :, :], self.wbr[l, b].rearrange("(c p) n -> p c n", p=128), w=['wst'])
                sc.op('pool', lambda b=b: nc.gpsimd.tensor_copy(out=wbr_b[:, b], in_=wst[:, :, :]), r=['wst'], w=['wbr_b'])
            for hf in range(2):
                sc.dma(wst[:, :, :], self.wout[l, hf * 512:(hf + 1) * 512, :].rearrange("(c p) n -> p c n", p=128),
                       w=['wst'])
                sc.op('pool', lambda hf=hf: nc.gpsimd.tensor_copy(out=wout_b[:, hf * 4:(hf + 1) * 4, :], in_=wst[:, :, :]),
                      r=['wst'], w=['wout_b'])
            cnt = 0
            for tq in range(S // 512):
                c0 = tq * 512
                for m in keep:
                    for c_ in range(4):
                        sc.dma(yt[:, m, c_, :], self.YT[m][c_ * 128:(c_ + 1) * 128, c0:c0 + 512], w=['yt'])
                        sc.dma(zt[:, m, c_, :], self.ZGT[m * 512 + c_ * 128:m * 512 + (c_ + 1) * 128, c0:c0 + 512], w=['zt'])
                for k_ in range(32):
                    if (k_ // 8) in keep:
                        sc.dma(gt[:, k_, :], self.ZGT[2048 + k_ * 128:2048 + (k_ + 1) * 128, c0:c0 + 512], w=['gt'])
                for m in keep:
                    sc.op('dve', lambda m=m: nc.vector.tensor_tensor(out=yt[:, m], in0=yt[:, m], in1=zt[:, m], op=ALU.mult),
                          r=['yt', 'zt'], w=['yt'])
                for oc in range(8):
                    for b in keep:
                        pi = b % 2
                        for c in range(4):
                            sc.op('pe', lambda pi=pi, b=b, c=c, oc=oc: nc.tensor.matmul(
                                P[pi][:, :], lhsT=wbr_b[:, b, c, oc * 128:(oc + 1) * 128], rhs=yt[:, b, c, :],
                                start=(c == 0), stop=(c == 3)), r=['wbr_b', 'yt'], w=['P%d' % pi], sig=(c == 3))
                        if b == keep[0]:
                            sc.op('dve', lambda pi=pi, oc=oc, b=b: nc.vector.tensor_tensor(
                                out=macc[:, :], in0=P[pi][:, :], in1=gt[:, b * 8 + oc, :], op=ALU.mult),
                                r=['P%d' % pi, 'gt'], w=['macc'])
                        else:
                            ti = b % 2
                            sc.op('dve', lambda pi=pi, oc=oc, b=b, ti=ti: nc.vector.tensor_tensor(
                                out=tmp[ti][:, :], in0=P[pi][:, :], in1=gt[:, b * 8 + oc, :], op=ALU.mult),
                                r=['P%d' % pi, 'gt'], w=['tmp%d' % ti])
                            sc.op('dve', lambda ti=ti: nc.vector.tensor_tensor(
                                out=macc[:, :], in0=macc[:, :], in1=tmp[ti][:, :], op=ALU.add),
                                r=['macc', 'tmp%d' % ti], w=['macc'])
                    sc.op('act', lambda oc=oc: nc.scalar.copy(out=mT[:, oc, :], in_=macc[:, :]), r=['macc'], w=['mT'])
                for ti in range(4):
                    i = cnt % 2
                    cnt += 1
                    r0 = c0 + ti * 128
                    sc.dma(xt[i][:, :], Xsrc[r0:r0 + 128, :], w=['xt%d' % i])
                    for hf in range(2):
                        pi = 2 + hf
                        for oc in range(8):
                            sc.op('pe', lambda pi=pi, oc=oc, ti=ti, hf=hf: nc.tensor.matmul(
                                P[pi][:, :], lhsT=mT[:, oc, ti * 128:(ti + 1) * 128], rhs=wout_b[:, oc, hf * 512:(hf + 1) * 512],
                                start=(oc == 0), stop=(oc == 7)), r=['mT', 'wout_b'], w=['P%d' % pi], sig=(oc == 7))
                        sc.op('dve', lambda pi=pi, hf=hf, i=i: nc.vector.tensor_tensor(
                            out=xo[i][:, hf * 512:(hf + 1) * 512], in0=P[pi][:, :], in1=xt[i][:, hf * 512:(hf + 1) * 512],
                            op=ALU.add), r=['P%d' % pi, 'xt%d' % i], w=['xo%d' % i])
                    sc.dma(Xdst[r0:r0 + 128, :], xo[i][:, :], r=['xo%d' % i], w=[('xd', r0)])
        sc.barrier()

    def layer(self, l, Xsrc, Xdst, mixers="ABCD"):
        self.phase1(l, Xsrc)
        self.sc.barrier()
        if "A" in mixers:
            self.phaseA(l)
        if "B" in mixers:
            self.phaseB(l)
        if "C" in mixers:
            self.phaseC(l)
        if "D" in mixers:
            self.phaseD(l)
        self.phase3(l, Xsrc, Xdst, tuple('ABCD'.index(c) for c in mixers))


    def finish(self):
        sc = self.sc
        sc.barrier()


def host_inputs(inputs, S, L):
    f = np.float32
    cols = _w_in_cols()
    w_in = np.asarray(inputs['w_in'], f)
    wpad = np.concatenate([w_in, np.zeros((L, 1024, 1), f)], axis=2)
    w_in_p = np.ascontiguousarray(wpad[:, :, np.where(cols >= 0, cols, w_in.shape[2])])
    wkvb = np.asarray(inputs['w_kv_b'], f).reshape(L, 128, 8, 2, 64).transpose(0, 1, 3, 2, 4).reshape(L, 128, 1024)
    pv = np.concatenate([
        np.asarray(inputs['norm_gain'], f).reshape(L, -1),
        np.asarray(inputs['qk_gain_a'], f).reshape(L, -1),
        np.asarray(inputs['qk_gain_b'], f).reshape(L, -1),
        np.asarray(inputs['qk_gain_c'], f).reshape(L, -1),
        np.asarray(inputs['qk_gain_d'], f).reshape(L, -1),
        np.asarray(inputs['c_q_gain'], f).reshape(L, -1),
        np.asarray(inputs['c_kv_gain'], f).reshape(L, -1),
        np.asarray(inputs['sinks'], f).reshape(L, -1)], axis=1)
    assert pv.shape[1] == NPV
    ii = np.arange(128)
    ident = np.eye(128, dtype=f)
    causneg = np.where(ii[None, :] <= ii[:, None], 0.0, -BIG).astype(f)
    caus_kq = np.where(ii[:, None] <= ii[None, :], 0.0, NEG).astype(f)
    pow2 = np.broadcast_to((2.0 ** -np.arange(32)).astype(f)[None, :], (128, 32))
    cst = np.ascontiguousarray(np.concatenate([ident, causneg, caus_kq, pow2], axis=1))
    half = 16
    freq = (np.float32(10000.0) ** (-np.arange(half, dtype=f) / half)).astype(f)
    ang = np.arange(S, dtype=f)[:, None] * freq[None, :]
    rope = np.concatenate([np.cos(ang), np.sin(ang)], axis=1).astype(f)
    rb = np.asarray(inputs['rel_bias'], f)
    kk, qq = ii[:, None], ii[None, :]
    bA = np.zeros((8, 128, 13, 128), f)
    for D in range(13):
        dist = 128 * D + qq - kk
        g = rb[_t5_bucket(dist)][:, :, 0:8]
        if D == 0:
            g = np.where((dist >= 0)[:, :, None], g, f(NEG))
        bA[:, :, D, :] = g.transpose(2, 0, 1)
    b31 = np.ascontiguousarray(np.broadcast_to(rb[31, 0:8][None, :], (128, 8)))
    def band(table, step, max_dist):
        out = np.zeros((128, table.shape[1], 2, 128), f)
        for pc in range(2):
            rel = (128 if pc == 0 else 0) + qq - kk
            ok = (rel >= 0) & (rel <= max_dist)
            g = table[_t5_bucket(rel * step)]
            g = np.where(ok[:, :, None], g, f(NEG))
            out[:, :, pc, :] = g.transpose(0, 2, 1)
        return out
    bB = np.stack([band(rb[:, 8 + g * 8: 16 + g * 8], B_DIL[g], 128) for g in range(3)], axis=1)
    bD = band(rb[:, 32:40], 1, 127)
    return dict(w_in=w_in_p, wqb=np.ascontiguousarray(np.asarray(inputs['w_q_b'], f)),
                wkvb=np.ascontiguousarray(wkvb), wbr=np.ascontiguousarray(np.asarray(inputs['w_branch'], f)),
                wout=np.ascontiguousarray(np.asarray(inputs['w_out'], f)), pv=np.ascontiguousarray(pv), cst=cst,
                rope=rope, biasA=np.ascontiguousarray(bA.reshape(8, 128, 13 * 128)), b31=b31,
                biasB=np.ascontiguousarray(bB.reshape(128, -1)), biasD=np.ascontiguousarray(bD.reshape(128, -1)))


_CACHE = {}


def _get_prog(S, L):
    key = (S, L)
    if key not in _CACHE:
        p = Prog(S, L)
        src = p.x_in
        for l in range(L):
            dst = p.y_out if l == L - 1 else p.X[l % 2]
            p.layer(l, src, dst)
            src = dst
        p.finish()
        _CACHE[key] = p
    return _CACHE[key]


def kernel(**inputs):
    x = np.asarray(inputs['x'], np.float32)
    B, S, _ = x.shape
    L = np.asarray(inputs['norm_gain']).shape[0]
    p = _get_prog(S, L)
    shared = host_inputs(inputs, S, L)
    n_cores = 8
    in_maps = []
    for c in range(n_cores):
        m = dict(shared)
        m['x'] = np.ascontiguousarray(x[c % B])
        in_maps.append(m)
    res = run_bass_kernel_spmd(p.nc, in_maps, core_ids=list(range(n_cores)))
    out = np.stack([np.asarray(res.results[b]['y'], np.float32) for b in range(B)], axis=0)
    return out
```
